# Optimizing a Trainium2 kernel written in Bass

```python
import jax
import jax.numpy as jnp
from jax import lax
import numpy as np

D_MODEL = 1024
BATCH = 4
SEQ = 4096
DEPTH = 4
DEC_BATCH = 16
DEC_SEQ = 2048
PAST_LEN = 128

HEAD_DIM = 64
N_HEADS = D_MODEL // HEAD_DIM
MIX_WIDTH = N_HEADS * HEAD_DIM
Q_PER_KV = 2
A_HEADS = N_HEADS // 4
B_HEADS = (3 * N_HEADS) // 8
C_HEADS = N_HEADS - A_HEADS - B_HEADS
A_KV = A_HEADS // Q_PER_KV
B_KV = B_HEADS // Q_PER_KV
C_KV = C_HEADS // Q_PER_KV
KV_HEADS = A_KV + B_KV + C_KV
QKV_WIDTH = (N_HEADS + 2 * KV_HEADS) * HEAD_DIM
FFN_DIM = ((-(-8 * D_MODEL // 3) + 255) // 256) * 256

GRID_W = 64
NA_MAX_ROWS = 8
NA_COLS = 16
DILATED_BRANCHES = ((128, 1), (512, 4), (2048, 16))
Q_BLOCK = 128
ROPE_THETA = 10000.0
RMS_EPS = 1e-6
NEG_INF = -1e30

kernel_name = 'hybrid_parallel_head_encoder'


def rms_norm(x, gain):
    xf = x.astype(jnp.float32)
    y = xf * lax.rsqrt(jnp.mean(xf * xf, axis=-1, keepdims=True) + RMS_EPS)
    return (y * gain.astype(jnp.float32)).astype(x.dtype)


def rope_tables(pos, dim):
    inv_freq = ROPE_THETA ** (-jnp.arange(0, dim, 2, dtype=jnp.float32) / dim)
    ang = pos[:, None] * inv_freq[None, :]
    ang = jnp.concatenate([ang, ang], axis=-1)
    return jnp.cos(ang), jnp.sin(ang)


def apply_rope(x, cos, sin):
    xf = x.astype(jnp.float32)
    half = xf.shape[-1] // 2
    rot = jnp.concatenate([-xf[..., half:], xf[..., :half]], axis=-1)
    return (xf * cos[None, :, None, :] + rot * sin[None, :, None, :]).astype(x.dtype)


def apply_axial_rope(x, cos_r, sin_r, cos_c, sin_c):
    half = x.shape[-1] // 2
    return jnp.concatenate([apply_rope(x[..., :half], cos_r, sin_r),
                            apply_rope(x[..., half:], cos_c, sin_c)], axis=-1)


def dense_block_attention(q, k, v):
    b, t, hq, d = q.shape
    hkv = k.shape[2]
    g = hq // hkv
    nb = t // Q_BLOCK
    scale = d ** -0.5
    qb = jnp.moveaxis(q.reshape(b, nb, Q_BLOCK, hkv, g, d), 1, 0)

    def block(qi):
        s = jnp.einsum('bqhgd,bkhd->bhgqk', qi, k, preferred_element_type=jnp.float32) * scale
        p = jax.nn.softmax(s, axis=-1).astype(v.dtype)
        return jnp.einsum('bhgqk,bkhd->bqhgd', p, v)

    out = lax.map(block, qb)
    return jnp.moveaxis(out, 0, 1).reshape(b, t, hq, d)


def gathered_block_attention(q, k, v, idx, bias):
    b, t, hq, d = q.shape
    hkv = k.shape[2]
    g = hq // hkv
    nb = t // Q_BLOCK
    n_keys = idx.shape[1]
    hb = bias.shape[0]
    bias_heads = (hkv, g) if hb == hq else (1, 1)
    scale = d ** -0.5
    qb = jnp.moveaxis(q.reshape(b, nb, Q_BLOCK, hkv, g, d), 1, 0)
    ib = idx.reshape(nb, Q_BLOCK, n_keys)
    bb = jnp.moveaxis(bias.reshape(hb, nb, Q_BLOCK, n_keys), 1, 0)

    def block(args):
        qi, ii, bi = args
        kg = k[:, ii]
        vg = v[:, ii]
        s = jnp.einsum('bqhgd,bqkhd->bhgqk', qi, kg, preferred_element_type=jnp.float32) * scale
        s = s + bi.reshape(bias_heads + (Q_BLOCK, n_keys))[None].astype(jnp.float32)
        lse = jax.nn.logsumexp(s, axis=-1)
        p = jnp.exp(s - lse[..., None]).astype(v.dtype)
        o = jnp.einsum('bhgqk,bqkhd->bqhgd', p, vg)
        return o, jnp.moveaxis(lse, 3, 1)

    out, lse = lax.map(block, (qb, ib, bb))
    out = jnp.moveaxis(out, 0, 1).reshape(b, t, hq, d)
    lse = jnp.moveaxis(lse, 0, 1).reshape(b, t, hq)
    return out, lse


def neighbourhood_pattern(t):
    rows = t // GRID_W
    kh = min(NA_MAX_ROWS, rows)
    pos = jnp.arange(t, dtype=jnp.int32)
    r, c = pos // GRID_W, pos % GRID_W
    r0 = jnp.clip(r - kh // 2, 0, rows - kh)
    c0 = jnp.clip(c - NA_COLS // 2, 0, GRID_W - NA_COLS)
    key_r = r0[:, None, None] + jnp.arange(kh, dtype=jnp.int32)[None, :, None]
    key_c = c0[:, None, None] + jnp.arange(NA_COLS, dtype=jnp.int32)[None, None, :]
    idx = (key_r * GRID_W + key_c).reshape(t, kh * NA_COLS)
    off_r = jnp.broadcast_to(key_r - r[:, None, None] + (NA_MAX_ROWS - 1), (t, kh, NA_COLS)).reshape(t, -1)
    off_c = jnp.broadcast_to(key_c - c[:, None, None] + (NA_COLS - 1), (t, kh, NA_COLS)).reshape(t, -1)
    return idx, off_r, off_c


def dilated_pattern(t, window, dilation):
    half = window // (2 * dilation)
    pos = jnp.arange(t, dtype=jnp.int32)
    keys = pos[:, None] + jnp.arange(-half, half + 1, dtype=jnp.int32)[None, :] * dilation
    valid = (keys >= 0) & (keys < t)
    bias = jnp.where(valid, 0.0, NEG_INF).astype(jnp.float32)[None]
    return jnp.clip(keys, 0, t - 1), bias


def encoder_trunk(x, norm_mix, w_in, q_gain, k_gain, rpb, out_gain, w_out, norm_ffn, w_gate_up, w_down):
    b, t, _ = x.shape
    pos = jnp.arange(t, dtype=jnp.int32)
    half_dim = HEAD_DIM // 2
    cos_r, sin_r = rope_tables((pos // GRID_W).astype(jnp.float32), half_dim)
    cos_c, sin_c = rope_tables((pos % GRID_W).astype(jnp.float32), half_dim)
    cos_1d, sin_1d = rope_tables(pos.astype(jnp.float32), HEAD_DIM)
    na_idx, na_dr, na_dc = neighbourhood_pattern(t)
    dil = [dilated_pattern(t, w, d) for (w, d) in DILATED_BRANCHES]
    a_w = A_HEADS * HEAD_DIM
    b_w = B_HEADS * HEAD_DIM
    k_off = MIX_WIDTH
    v_off = MIX_WIDTH + KV_HEADS * HEAD_DIM
    for l in range(DEPTH):
        h = rms_norm(x, norm_mix[l])
        proj = h @ w_in[l]
        q = proj[..., :k_off].reshape(b, t, N_HEADS, HEAD_DIM)
        k = proj[..., k_off:v_off].reshape(b, t, KV_HEADS, HEAD_DIM)
        v = proj[..., v_off:].reshape(b, t, KV_HEADS, HEAD_DIM)

        q_a = apply_axial_rope(rms_norm(q[:, :, :A_HEADS], q_gain[l, 0]), cos_r, sin_r, cos_c, sin_c)
        k_a = apply_axial_rope(rms_norm(k[:, :, :A_KV], k_gain[l, 0]), cos_r, sin_r, cos_c, sin_c)
        o_a = dense_block_attention(q_a, k_a, v[:, :, :A_KV])

        q_b = rms_norm(q[:, :, A_HEADS:A_HEADS + B_HEADS], q_gain[l, 1])
        k_b = rms_norm(k[:, :, A_KV:A_KV + B_KV], k_gain[l, 1])
        bias_b = rpb[l][:, na_dr, na_dc]
        o_b, _ = gathered_block_attention(q_b, k_b, v[:, :, A_KV:A_KV + B_KV], na_idx, bias_b)

        q_c = apply_rope(rms_norm(q[:, :, A_HEADS + B_HEADS:], q_gain[l, 2]), cos_1d, sin_1d)
        k_c = apply_rope(rms_norm(k[:, :, A_KV + B_KV:], k_gain[l, 2]), cos_1d, sin_1d)
        v_c = v[:, :, A_KV + B_KV:]
        branches = [gathered_block_attention(q_c, k_c, v_c, idx, bias) for (idx, bias) in dil]
        outs = jnp.stack([o for (o, _) in branches]).astype(jnp.float32)
        lses = jnp.stack([s for (_, s) in branches])
        wts = jax.nn.softmax(lses, axis=0)
        o_c = jnp.einsum('nbth,nbthd->bthd', wts, outs).astype(x.dtype)

        y = jnp.concatenate([
            rms_norm(o_a.reshape(b, t, a_w), out_gain[l, :a_w]),
            rms_norm(o_b.reshape(b, t, b_w), out_gain[l, a_w:a_w + b_w]),
            rms_norm(o_c.reshape(b, t, MIX_WIDTH - a_w - b_w), out_gain[l, a_w + b_w:]),
        ], axis=-1)
        x = x + y @ w_out[l]

        h = rms_norm(x, norm_ffn[l])
        gate, up = jnp.split(h @ w_gate_up[l], 2, axis=-1)
        x = x + (jax.nn.silu(gate) * up) @ w_down[l]
    return x


def setup_inputs(seed: int = 0) -> dict:
    key = jax.random.key(seed)
    ks = jax.random.split(key, 12)
    nrm = jax.random.normal
    return {
        'x_prompt': nrm(ks[0], (BATCH, SEQ, D_MODEL), jnp.float32),
        'x_sample': nrm(ks[1], (DEC_BATCH, DEC_SEQ, D_MODEL), jnp.float32),
        'norm_mix': 1.0 + 0.02 * nrm(ks[2], (DEPTH, D_MODEL), jnp.float32),
        'w_in': nrm(ks[3], (DEPTH, D_MODEL, QKV_WIDTH), jnp.float32) * D_MODEL ** -0.5,
        'q_gain': 1.0 + 0.02 * nrm(ks[4], (DEPTH, 3, HEAD_DIM), jnp.float32),
        'k_gain': 1.0 + 0.02 * nrm(ks[5], (DEPTH, 3, HEAD_DIM), jnp.float32),
        'rpb': 0.1 * nrm(ks[6], (DEPTH, B_HEADS, 2 * NA_MAX_ROWS - 1, 2 * NA_COLS - 1), jnp.float32),
        'out_gain': 1.0 + 0.02 * nrm(ks[7], (DEPTH, MIX_WIDTH), jnp.float32),
        'w_out': nrm(ks[8], (DEPTH, MIX_WIDTH, D_MODEL), jnp.float32) * MIX_WIDTH ** -0.5,
        'norm_ffn': 1.0 + 0.02 * nrm(ks[9], (DEPTH, D_MODEL), jnp.float32),
        'w_gate_up': nrm(ks[10], (DEPTH, D_MODEL, 2 * FFN_DIM), jnp.float32) * D_MODEL ** -0.5,
        'w_down': nrm(ks[11], (DEPTH, FFN_DIM, D_MODEL), jnp.float32) * FFN_DIM ** -0.5,
    }


def reference(x_prompt, x_sample, norm_mix, w_in, q_gain, k_gain, rpb, out_gain, w_out, norm_ffn, w_gate_up, w_down):
    y_prompt = encoder_trunk(x_prompt, norm_mix, w_in, q_gain, k_gain, rpb, out_gain, w_out, norm_ffn, w_gate_up, w_down)
    y_sample = encoder_trunk(x_sample, norm_mix, w_in, q_gain, k_gain, rpb, out_gain, w_out, norm_ffn, w_gate_up, w_down)
    return (y_prompt, y_sample)
```

```python
import contextlib
import numpy as np
import ml_dtypes
import concourse.bass as bass
import concourse.mybir as mybir
from concourse.bass_utils import run_bass_kernel_spmd

F32 = mybir.dt.float32
BF = mybir.dt.bfloat16
AF = mybir.ActivationFunctionType
ALU = mybir.AluOpType

D = 1024
DEPTH = 4
NTOK = 6144
TP, TS = 4096, 2048
FFN = 2816
NJ = FFN // 128
EPS = 1e-6
SCALE = 0.125
SLOTS = ((0, TP), (TP, TS))
CHUNK_TYPE = {0: 0, 1: 0, 5: 1, 6: 1, 7: 1, 8: 0, 10: 2, 11: 1}
B_SPECIAL = {TP: (0, 7, 8, 15), TS: (0, 7)}
N_LAYERS_BUILD = DEPTH


class Prog:
    COMPUTE = ("pe", "act", "dve", "pool")
    EPOCH = 20000

    def __init__(self, nc):
        self.nc = nc
        self.ops = []
        self.last_w = {}
        self.readers = {}
        self.dma_count = {}

    def _add(self, eng, fn, reads, writes, dma_key=None):
        i = len(self.ops)
        deps = set()
        for b in reads:
            j = self.last_w.get(b)
            if j is not None:
                deps.add(j)
        for b in writes:
            j = self.last_w.get(b)
            if j is not None:
                deps.add(j)
            for j in self.readers.get(b, {}).values():
                deps.add(j)
        val = None
        if dma_key is not None:
            self.dma_count[dma_key] = self.dma_count.get(dma_key, 0) + 1
            val = 16 * self.dma_count[dma_key]
        self.ops.append([eng, fn, deps, dma_key, val, False])
        for b in writes:
            self.last_w[b] = i
            self.readers[b] = {}
        rk = eng if dma_key is None else ("dma", i)
        for b in reads:
            self.readers.setdefault(b, {})[rk] = i
        return i

    def op(self, eng, fn, reads=(), writes=()):
        return self._add(eng, fn, tuple(reads), tuple(writes))

    def dma(self, queue, fn, reads=(), writes=(), key=None):
        return self._add(queue, fn, tuple(reads), tuple(writes), dma_key=key)

    def emit(self, stack):
        nc = self.nc
        ops = self.ops
        for o in ops:
            for j in o[2]:
                pj = ops[j]
                if pj[3] is None:
                    if pj[0] == "pe" and o[0] == "pe" and o[3] is None:
                        continue
                    pj[5] = True
        sems = {}
        cnt = {e: 0 for e in self.COMPUTE}
        sig = {}
        for i, o in enumerate(ops):
            if o[3] is None:
                if o[5]:
                    cnt[o[0]] += 1
                    ep, v = divmod(cnt[o[0]] - 1, self.EPOCH)
                    sig[i] = ((o[0], ep), v + 1)
            else:
                sig[i] = (("dma", o[3]), o[4])
        for sk, _ in sig.values():
            if sk not in sems:
                sems[sk] = stack.enter_context(nc.semaphore("s%d" % len(sems)))
        per_eng = {}
        for i, o in enumerate(ops):
            per_eng.setdefault(o[0], []).append(i)
        engmap = {"pe": "tensor", "act": "scalar", "dve": "vector", "pool": "gpsimd", "sp": "sync"}
        final_waits = [(sems[("dma", k)], 16 * c) for k, c in self.dma_count.items()]
        block = stack.enter_context(nc.Block())

        def make(ename, idxs):
            def body(eng):
                waited = {}
                for i in idxs:
                    o = ops[i]
                    need = {}
                    for j in o[2]:
                        if j not in sig:
                            continue
                        sk, v = sig[j]
                        if v > need.get(sk, 0):
                            need[sk] = v
                    for sk, v in need.items():
                        if waited.get(sk, 0) >= v:
                            continue
                        eng.wait_ge(sems[sk], v)
                        waited[sk] = v
                    ins = o[1](eng)
                    if i in sig:
                        sk, v = sig[i]
                        ins.then_inc(sems[sk], 16 if o[3] is not None else 1)
                if ename == "sp":
                    for s, v in final_waits:
                        eng.wait_ge(s, v)
            return body

        if "sp" not in per_eng:
            per_eng["sp"] = []
        for ename, idxs in per_eng.items():
            getattr(block, engmap[ename])(make(ename, idxs))


class Rot:
    def __init__(self, items):
        self.items = list(items)
        self.i = -1

    def next(self):
        self.i = (self.i + 1) % len(self.items)
        return self.items[self.i]


def build_nc():
    nc = bass.Bass("TRN2", target_bir_lowering=False)

    def din(name, shape, dt=F32):
        return nc.dram_tensor(name, list(shape), dt, kind="ExternalInput").ap()

    def dscr(name, shape, dt):
        return nc.dram_tensor(name, list(shape), dt, kind="Internal").ap()

    xT = din("xT", [D, NTOK])
    yT = nc.dram_tensor("yT", [D, NTOK], F32, kind="ExternalOutput").ap()
    w_in_t = din("w_in_t", [DEPTH, 12, 128, 8 * 128])
    w_v_t = din("w_v_t", [DEPTH, 2, 128, 8 * 256])
    w_out_t = din("w_out_t", [DEPTH, 8, 128, 8 * 128])
    w_gu_t = din("w_gu_t", [DEPTH, NJ, 128, 8 * 256])
    w_dn_t = din("w_dn_t", [DEPTH, 8, 128, NJ * 128])
    gains_d = din("gains", [128, 3 * 32 + 48])
    rope_d = din("rope", [128, 6, NTOK])
    rperm_d = din("rperm", [128, 3 * 128], BF)
    bbank_d = din("bbank", [DEPTH, 6, 128, 2 * 960])
    bmask_d = din("bmask", [128, 36 * 256], BF)
    cmask_d = din("cmask", [128, 18 * 512], BF)
    cross_d = din("cross", [128, 1])

    wb_in = dscr("wb_in", [DEPTH, 12, 128, 8 * 128], BF)
    wb_v = dscr("wb_v", [DEPTH, 2, 128, 8 * 256], BF)
    wb_out = dscr("wb_out", [DEPTH, 8, 128, 8 * 128], BF)
    wb_gu = dscr("wb_gu", [DEPTH, NJ, 128, 8 * 256], BF)
    wb_dn = dscr("wb_dn", [DEPTH, 8, 128, NJ * 128], BF)
    xres = dscr("xres", [D, NTOK], F32)
    qkT = dscr("qkT", [12 * 128, NTOK], BF)
    Vd = dscr("Vd", [NTOK, 512], BF)
    oT = dscr("oT", [D, NTOK], F32)

    st = contextlib.ExitStack()
    with st:
        def sb(name, shape, dt):
            return st.enter_context(nc.sbuf_tensor("sb_" + name, list(shape), dt))

        def psum(name):
            return st.enter_context(nc.psum_tensor(name, [128, 512], F32))

        P = Prog(nc)
        uid = [0]

        def U(prefix):
            uid[0] += 1
            return (prefix, uid[0])

        gains = sb("gains", [128, 144], F32)
        rperm = sb("rperm", [128, 3, 128], BF)
        ones_bf = sb("ones_bf", [128, 128], BF)
        bones = sb("bones", [128, 128], BF)
        sel65 = sb("sel65", [65, 64], F32)
        cmask = sb("cmask", [128, 18, 512], BF)
        bmk = sb("bmk", [128, 6, 256], BF)

        P.dma("sp", lambda e: e.dma_start(out=gains[:], in_=gains_d), writes=["gains"], key="c_gains")
        P.dma("sp", lambda e: e.dma_start(out=rperm[:], in_=rperm_d.rearrange("p (a b) -> p a b", a=3)),
              writes=["rperm"], key="c_rperm")
        P.dma("sp", lambda e: e.dma_start(out=cmask[:], in_=cmask_d.rearrange("p (a b) -> p a b", a=18)),
              writes=["cmask"], key="c_cmask")
        P.op("pool", lambda e: e.memset(ones_bf[:], 1.0), writes=["ones_bf"])
        P.op("pool", lambda e: e.memset(bones[:], 0.0), writes=["bones"])
        P.op("pool", lambda e: e.memset(bones[0:64, 0:64], 1.0), reads=[], writes=["bones"])
        P.op("pool", lambda e: e.memset(bones[64:128, 64:128], 1.0), reads=[], writes=["bones"])
        P.op("pool", lambda e: e.memset(sel65[:], 0.0), writes=["sel65"])
        P.op("pool", lambda e: e.memset(sel65[64:65, :], 1.0), writes=["sel65"])

        def g_mix(l, k):
            return gains[:, l * 8 + k: l * 8 + k + 1]

        def g_ffn(l, k):
            return gains[:, 32 + l * 8 + k: 32 + l * 8 + k + 1]

        def g_out(l, k):
            return gains[:, 64 + l * 8 + k: 64 + l * 8 + k + 1]

        def g_qk(l, c):
            return gains[:, 96 + l * 12 + c: 96 + l * 12 + c + 1]

        def cast_w(src, dst, l, name):
            s2 = src[l]
            d2 = dst[l]
            nd = len(s2.shape)
            if nd == 3:
                s2 = s2.rearrange("a p n -> (a p) n")
                d2 = d2.rearrange("a p n -> (a p) n")
            rows = s2.shape[0]
            step = 512
            for r0 in range(0, rows, step):
                r1 = min(rows, r0 + step)
                P.dma("pool", lambda e, a=d2[r0:r1, :], b=s2[r0:r1, :]: e.dma_start(out=a, in_=b),
                      writes=[(name, l, r0 // 128 + i) for i in range((r1 - r0 + 127) // 128)],
                      key=("cast", name, l, r0))

        for l in range(N_LAYERS_BUILD):
            cast_w(w_in_t, wb_in, l, "wb_in")
            cast_w(w_v_t, wb_v, l, "wb_v")
            cast_w(w_out_t, wb_out, l, "wb_out")
            cast_w(w_gu_t, wb_gu, l, "wb_gu")
            cast_w(w_dn_t, wb_dn, l, "wb_dn")

        bank = [("ps%d" % i, psum("ps%d" % i)) for i in range(8)]
        psA = Rot(bank[0:3])
        psB = Rot(bank[3:5])
        psZ = bank[5]
        psN = bank[6]
        psH = bank[7]
        psS = Rot([bank[0], bank[1], bank[2], bank[6], bank[7]])

        xa = Rot([("xa%d" % i, sb("xa%d" % i, [128, 8, 512], F32)) for i in range(2)])
        hTr = Rot([("hT%d" % i, sb("hT%d" % i, [128, 8, 512], BF)) for i in range(2)])
        ropeR = Rot([("rope%d" % i, sb("rope%d" % i, [128, 6, 512], F32)) for i in range(2)])
        wsl = Rot([("w%d" % i, sb("w%d" % i, [128, 8 * 256], BF)) for i in range(3)])
        wdn = Rot([("wd%d" % i, sb("wd%d" % i, [128, NJ * 128], BF)) for i in range(2)])
        actT = sb("actT", [128, NJ, 512], BF)
        sqb = Rot([("sqb%d" % i, sb("sqb%d" % i, [128, 512], BF)) for i in range(2)])
        tf = Rot([("tf%d" % i, sb("tf%d" % i, [128, 512], F32)) for i in range(5)])
        tbe = Rot([("tbe%d" % i, sb("tbe%d" % i, [128, 512], BF)) for i in range(2)])
        tb = Rot([("tb%d" % i, sb("tb%d" % i, [128, 512], BF)) for i in range(5)])
        rstd3 = [("rs%d" % i, sb("rs%d" % i, [128, 512], F32)) for i in range(3)]
        sqc = Rot([("sqc%d" % i, sb("sqc%d" % i, [128, 512], BF)) for i in range(2)])
        stq = Rot([("stq%d" % i, sb("stq%d" % i, [128, 512], BF)) for i in range(2)])
        stv = Rot([("stv%d" % i, sb("stv%d" % i, [128, 4, 512], BF)) for i in range(1)])
        KT = Rot([("KT%d" % i, sb("KT%d" % i, [128, TP], BF)) for i in range(1)])
        QT = Rot([("QT%d" % i, sb("QT%d" % i, [128, 256], BF)) for i in range(3)])
        Vt = Rot([("Vt%d" % i, sb("Vt%d" % i, [128, 32, 65], BF)) for i in range(1)])
        ebR = Rot([("eb%d" % i, sb("eb%d" % i, [128, 2, 960], F32)) for i in range(2)])
        uf = Rot([("uf%d" % i, sb("uf%d" % i, [65, 512], F32)) for i in range(2)])
        ost = Rot([("ost%d" % i, sb("ost%d" % i, [64, 512], F32)) for i in range(1)])

        crossb = sb("crossb", [128, 1], F32)
        P.dma("sp", lambda e: e.dma_start(out=crossb[:], in_=cross_d), writes=["crossb"], key="c_crossb")
        for (vn, vt_) in Vt.items:
            P.op("pool", lambda e, t=vt_: e.memset(t[:, :, 64:65], 1.0), writes=[vn])

        class Pipe:
            def __init__(self):
                self.step = 0
                self.q = []
                self.seq = 0

            def defer(self, lag, fn):
                self.q.append((self.step + lag, self.seq, fn))
                self.seq += 1

            def tick(self):
                self.step += 1
                self.flush()

            def flush(self, everything=False):
                while True:
                    ready = [x for x in self.q if everything or x[0] <= self.step]
                    if not ready:
                        break
                    ready.sort()
                    x = ready[0]
                    self.q.remove(x)
                    x[2]()

        pipe = Pipe()

        def load_w(rot, src2d, deps, ncols):
            name, t = rot.next()
            P.dma("sp", lambda e, t=t, s=src2d, n=ncols: e.dma_start(out=t[:, 0:n], in_=s),
                  reads=deps, writes=[name], key="l_" + name)
            return name, t

        def recip(out_ap, in_ap, reads, writes, npart=128, n=512):
            P.op("dve", lambda e: e.reciprocal(out=out_ap, in_=in_ap), reads=list(reads), writes=list(writes))

        def rms_rstd(src_name, src_t, nk, groups, inv_dims):
            outs = []
            for gi, grp in enumerate(groups):
                pn, pt = psN
                for ii, k in enumerate(grp):
                    sn, s_ = sqb.next()
                    P.op("act", lambda e, s_=s_, k=k: e.activation(out=s_[:], in_=src_t[:, k, :], func=AF.Square),
                         reads=[(src_name, k)], writes=[sn])
                    P.op("pe", lambda e, s_=s_, a=(ii == 0), b=(ii == len(grp) - 1), pt=pt:
                         e.matmul(pt[:], ones_bf[:], s_[:], start=a, stop=b),
                         reads=[sn, "ones_bf"], writes=[pn])
                rn, rt = rstd3[gi]
                tn, tt_ = tf.next()
                P.op("act", lambda e, tt_=tt_, pt=pt, sc=inv_dims[gi]:
                     e.activation(out=tt_[:], in_=pt[:], func=AF.Sqrt, scale=sc, bias=EPS),
                     reads=[pn], writes=[tn])
                recip(rt[:], tt_[:], [tn], [rn])
                outs.append((rn, rt))
            return outs

        for l in range(N_LAYERS_BUILD):
            src = xT if l == 0 else xres
            dst = yT if l == N_LAYERS_BUILD - 1 else xres
            src_key = "xin" if l == 0 else "xres"
            dst_key = "yT" if l == N_LAYERS_BUILD - 1 else "xres"

            def prep(tc, l=l, src=src, src_key=src_key):
                t0 = tc * 512
                xn, xt = xa.next()
                P.dma("sp", lambda e, xt=xt, t0=t0, src=src: e.dma_start(
                    out=xt[:], in_=src[:, t0:t0 + 512].rearrange("(k p) t -> p k t", p=128)),
                    reads=[(src_key, tc)], writes=[(xn, k) for k in range(8)], key="l_" + xn)
                rpn, rpt = ropeR.next()
                P.dma("sp", lambda e, t0=t0, rpt=rpt: e.dma_start(out=rpt[:], in_=rope_d[:, :, t0:t0 + 512]),
                      writes=[rpn], key="l_" + rpn)
                (rn, rt), = rms_rstd(xn, xt, 8, [list(range(8))], [1.0 / D])
                hn, ht = hTr.next()
                for k in range(8):
                    P.op("dve", lambda e, k=k, xt=xt, rt=rt, l=l, ht=ht: e.scalar_tensor_tensor(
                        out=ht[:, k, :], in0=xt[:, k, :], scalar=g_mix(l, k), in1=rt[:],
                        op0=ALU.mult, op1=ALU.mult),
                        reads=[(xn, k), rn, "gains"], writes=[(hn, k)])
                return hn, ht, rpn, rpt

            def qk_chunk(tc, c, hn, ht, rpn, rpt, l=l):
                t0 = tc * 512
                wn, wt = load_w(wsl, wb_in[l, c], [("wb_in", l, c)], 1024)
                pn, pt = psA.next()
                for k in range(8):
                    P.op("pe", lambda e, k=k, wt=wt, pt=pt: e.matmul(
                        pt[:], wt[:, k * 128:(k + 1) * 128], ht[:, k, :], start=(k == 0), stop=(k == 7)),
                        reads=[wn, (hn, k)], writes=[pn])
                sn, s_ = sqc.next()
                P.op("act", lambda e, s_=s_, pt=pt: e.activation(out=s_[:], in_=pt[:], func=AF.Square),
                     reads=[pn], writes=[sn])
                ty = CHUNK_TYPE.get(c)
                state = {}

                def part2():
                    hhn, hht = psH
                    P.op("pe", lambda e, s_=s_, hht=hht: e.matmul(hht[:], bones[:], s_[:], start=True, stop=True),
                         reads=[sn, "bones"], writes=[hhn])
                    t1n, t1 = tf.next()
                    P.op("act", lambda e, t1=t1, hht=hht: e.activation(out=t1[:], in_=hht[:], func=AF.Sqrt,
                                                                       scale=1.0 / 64, bias=EPS),
                         reads=[hhn], writes=[t1n])
                    r2n, r2 = tf.next()
                    recip(r2[:], t1[:], [t1n], [r2n])
                    qsn, qs = stq.next()
                    state["qs"] = (qsn, qs)
                    if ty is None:
                        P.op("dve", lambda e, qs=qs, r2=r2: e.scalar_tensor_tensor(
                            out=qs[:], in0=pt[:], scalar=g_qk(l, c), in1=r2[:], op0=ALU.mult, op1=ALU.mult),
                            reads=[pn, r2n, "gains"], writes=[qsn])
                        store(qsn, qs)
                    else:
                        qfn, qf = tf.next()
                        P.op("dve", lambda e, qf=qf, r2=r2: e.scalar_tensor_tensor(
                            out=qf[:], in0=pt[:], scalar=g_qk(l, c), in1=r2[:], op0=ALU.mult, op1=ALU.mult),
                            reads=[pn, r2n, "gains"], writes=[qfn])
                        qbn, qb_ = tb.next()
                        P.op("pool", lambda e, qb_=qb_, qf=qf: e.tensor_copy(out=qb_[:], in_=qf[:]),
                             reads=[qfn], writes=[qbn])
                        an, a_ = tf.next()
                        P.op("pool", lambda e, a_=a_, qf=qf: e.tensor_tensor(
                            out=a_[:], in0=qf[:], in1=rpt[:, 2 * ty, :], op=ALU.mult),
                            reads=[qfn, rpn], writes=[an])
                        state["rot"] = (qbn, qb_, an, a_)

                def store(qsn, qs):
                    P.dma("pool", lambda e, qs=qs: e.dma_start(
                        out=qkT[c * 128:(c + 1) * 128, t0:t0 + 512], in_=qs[:]),
                        reads=[qsn], writes=[("qkT", c, tc)], key="s_" + qsn)

                def part3():
                    qbn, qb_, an, a_ = state["rot"]
                    qsn, qs = state["qs"]
                    rbn, rb = psB.next()
                    P.op("pe", lambda e, rb=rb, qb_=qb_: e.matmul(rb[:], rperm[:, ty, :], qb_[:], start=True, stop=True),
                         reads=[qbn, "rperm"], writes=[rbn])
                    bn, b_ = tf.next()
                    P.op("dve", lambda e, b_=b_, rb=rb: e.tensor_tensor(
                        out=b_[:], in0=rb[:], in1=rpt[:, 2 * ty + 1, :], op=ALU.mult),
                        reads=[rbn, rpn], writes=[bn])
                    P.op("pool", lambda e, qs=qs, a_=a_, b_=b_: e.tensor_tensor(
                        out=qs[:], in0=a_[:], in1=b_[:], op=ALU.add),
                        reads=[an, bn], writes=[qsn])
                    store(qsn, qs)

                pipe.defer(2, part2)
                if ty is not None:
                    pipe.defer(3, part3)
                pipe.tick()

            def v_proj(tc, hn, ht, l=l):
                t0 = tc * 512
                svn, sv = stv.next()
                for half in range(2):
                    wn, wt = load_w(wsl, wb_v[l, half], [("wb_v", l, half)], 2048)
                    for tt in range(4):
                        pn, pt = psA.next()
                        for k in range(8):
                            P.op("pe", lambda e, k=k, tt=tt, pt=pt, wt=wt: e.matmul(
                                pt[:, 0:256], ht[:, k, tt * 128:(tt + 1) * 128], wt[:, k * 256:(k + 1) * 256],
                                start=(k == 0), stop=(k == 7)),
                                reads=[wn, (hn, k)], writes=[pn])
                        P.op("act", lambda e, sv=sv, tt=tt, pt=pt, half=half: e.activation(
                            out=sv[:, tt, half * 256:(half + 1) * 256], in_=pt[:, 0:256], func=AF.Copy),
                            reads=[pn], writes=[(svn, tt, half)])
                        pipe.tick()
                P.dma("pool", lambda e, sv=sv: e.dma_start(
                    out=Vd[t0:t0 + 512, :].rearrange("(tt p) n -> p tt n", p=128), in_=sv[:]),
                    reads=[(svn, tt, hf) for tt in range(4) for hf in range(2)], writes=[("Vd", tc)], key="s_" + svn)

            ntc = NTOK // 512
            cur = prep(0)
            for tc in range(ntc):
                hn, ht, rpn, rpt = cur
                for c in range(12):
                    qk_chunk(tc, c, hn, ht, rpn, rpt)
                    if c == 5 and tc + 1 < ntc:
                        cur = prep(tc + 1)
                v_proj(tc, hn, ht)
            pipe.flush(everything=True)

            def attend(base, T, g, heads, mixer, l=l):
                ntile = T // 128
                tcs = range(base // 512, (base + T) // 512)
                pipe.flush(everything=True)
                kn, kt_ = KT.next()
                for hh in range(2):
                    P.dma("sp", lambda e, kt_=kt_, hh=hh: e.dma_start(
                        out=kt_[hh * 64:(hh + 1) * 64, 0:T],
                        in_=qkT[1024 + g * 64:1024 + (g + 1) * 64, base:base + T]),
                        reads=[("qkT", 8 + g // 2, tc) for tc in tcs], writes=[kn], key="l_" + kn)
                vn, vt_ = Vt.next()
                P.dma("sp", lambda e, vt_=vt_: e.dma_start(
                    out=vt_[:, 0:ntile, 0:64],
                    in_=Vd[base:base + T, g * 64:(g + 1) * 64].rearrange("(n p) d -> p n d", p=128)),
                    reads=[("Vd", tc) for tc in tcs], writes=[vn], key="l_" + vn)
                ebs = []
                if mixer == 1:
                    for h in heads:
                        ebn, ebt = ebR.next()
                        P.dma("sp", lambda e, h=h, ebt=ebt: e.dma_start(
                            out=ebt[:], in_=bbank_d[l, h - 4].rearrange("p (a b) -> p a b", a=2)),
                            writes=[ebn], key="l_" + ebn)
                        P.op("act", lambda e, ebt=ebt: e.activation(out=ebt[:], in_=ebt[:], func=AF.Exp),
                             reads=[ebn], writes=[ebn])
                        ebs.append((ebn, ebt))
                N = 256
                for qb in range(T // N):
                    q0 = qb * N
                    qn, qt_ = QT.next()
                    P.dma("sp", lambda e, qt_=qt_, q0=q0: e.dma_start(
                        out=qt_[:, 0:N], in_=qkT[g * 128:(g + 1) * 128, base + q0:base + q0 + N]),
                        reads=[("qkT", g, (base + q0) // 512)], writes=[qn], key="l_" + qn)
                    tiles = []
                    if mixer == 0:
                        for kt in range(ntile):
                            cross = (T == TP) and ((q0 < TS) != (kt < 16))
                            tiles.append((kt, cross, None, None))
                    elif mixer == 2:
                        for kt in range(ntile):
                            o = kt - 2 * qb
                            if -8 <= o <= 9:
                                cross = (T == TP) and ((q0 < TS) != (kt < 16))
                                tiles.append((kt, cross, ("c", o + 8), None))
                    else:
                        spec = B_SPECIAL[T]
                        is_spec = qb in spec
                        if is_spec:
                            mi = (0 if T == TP else 4) + spec.index(qb)
                            P.dma("sp", lambda e, mi=mi: e.dma_start(
                                out=bmk[:], in_=bmask_d[:, mi * 6 * 256:(mi + 1) * 6 * 256].rearrange(
                                    "p (a b) -> p a b", a=6)),
                                writes=["bmk"], key="l_bmk")
                        for t in range(6):
                            kt = 2 * qb - 2 + t
                            if 0 <= kt < ntile:
                                tiles.append((kt, False, ("b", 1 if is_spec else 0, t), t if is_spec else None))
                    un, ut = psB.next()
                    for ti, (kt, cross, m1, m2) in enumerate(tiles):
                        sc = [psS.next(), psS.next()]
                        for hh in range(2):
                            P.op("pe", lambda e, s_=sc[hh][1], kt=kt, qt_=qt_, hh=hh: e.matmul(
                                s_[:, 0:256],
                                kt_[hh * 64:(hh + 1) * 64, kt * 128:(kt + 1) * 128],
                                qt_[hh * 64:(hh + 1) * 64, 0:256],
                                start=True, stop=True),
                                reads=[kn, qn], writes=[sc[hh][0]])
                        pbn, pb = tb.next()
                        if m1 is None:
                            dstn, dstt = pbn, pb
                        elif m1[0] == "c":
                            dstn, dstt = tbe.next()
                        else:
                            dstn, dstt = tf.next()
                        for hh in range(2):
                            if cross:
                                P.op("act", lambda e, dstt=dstt, s_=sc[hh][1], hh=hh: e.activation(
                                    out=dstt[:, hh * 256:(hh + 1) * 256], in_=s_[:, 0:256], func=AF.Exp,
                                    scale=SCALE, bias=crossb[:, 0:1]),
                                    reads=[sc[hh][0], "crossb"], writes=[dstn])
                            else:
                                P.op("act", lambda e, dstt=dstt, s_=sc[hh][1], hh=hh: e.activation(
                                    out=dstt[:, hh * 256:(hh + 1) * 256], in_=s_[:, 0:256], func=AF.Exp,
                                    scale=SCALE),
                                    reads=[sc[hh][0]], writes=[dstn])
                        if m1 is not None:
                            ef = dstt
                            efn = dstn
                            if m1[0] == "c":
                                P.op("dve", lambda e, pb=pb, ef=ef, oi=m1[1]: e.tensor_tensor(
                                    out=pb[:], in0=ef[:], in1=cmask[:, oi, :], op=ALU.mult),
                                    reads=[efn, "cmask"], writes=[pbn])
                            else:
                                v, t = m1[1], m1[2]
                                i0 = (11 - 2 * t) * 64
                                if m2 is None:
                                    for hh in range(2):
                                        ebn, ebt = ebs[hh]
                                        P.op("pool" if hh == 0 else "dve", lambda e, pb=pb, ef=ef, v=v, i0=i0, hh=hh, ebt=ebt: e.tensor_tensor(
                                            out=pb[:, hh * 256:(hh + 1) * 256], in0=ef[:, hh * 256:(hh + 1) * 256],
                                            in1=ebt[:, v, i0:i0 + 256], op=ALU.mult),
                                            reads=[efn, ebn], writes=[(pbn, "h", hh)])
                                else:
                                    e2n, e2 = tf.next()
                                    for hh in range(2):
                                        ebn, ebt = ebs[hh]
                                        P.op("dve", lambda e, e2=e2, ef=ef, v=v, i0=i0, hh=hh, ebt=ebt: e.tensor_tensor(
                                            out=e2[:, hh * 256:(hh + 1) * 256], in0=ef[:, hh * 256:(hh + 1) * 256],
                                            in1=ebt[:, v, i0:i0 + 256], op=ALU.mult),
                                            reads=[efn, ebn], writes=[e2n])
                                    for hh in range(2):
                                        P.op("pool", lambda e, pb=pb, e2=e2, t=m2, hh=hh: e.tensor_tensor(
                                            out=pb[:, hh * 256:(hh + 1) * 256], in0=e2[:, hh * 256:(hh + 1) * 256],
                                            in1=bmk[:, t, :], op=ALU.mult),
                                            reads=[e2n, "bmk"], writes=[pbn])

                        def pv(ut=ut, un=un, kt=kt, pb=pb, pbn=pbn, a=(ti == 0), b=(ti == len(tiles) - 1)):
                            P.op("pe", lambda e: e.matmul(
                                ut[0:65, :], vt_[:, kt, :], pb[:], start=a, stop=b),
                                reads=[vn, pbn, (pbn, "h", 0), (pbn, "h", 1)], writes=[un])
                        pipe.defer(4, pv)
                        pipe.tick()

                    def fin1(ut=ut, un=un, q0=q0):
                        ufn, uft = uf.next()
                        P.op("act", lambda e: e.activation(out=uft[:], in_=ut[0:65, :], func=AF.Copy),
                             reads=[un], writes=[ufn])

                        def fin2():
                            zn, zt = psZ
                            P.op("pe", lambda e: e.matmul(zt[0:64, :], sel65[:], uft[:], start=True, stop=True),
                                 reads=[ufn, "sel65"], writes=[zn])
                            rzn, rz = tf.next()
                            recip(rz[0:64, :], zt[0:64, :], [zn], [rzn], npart=64, n=512)
                            on, ot = ost.next()
                            P.op("pool", lambda e: e.tensor_tensor(
                                out=ot[:], in0=uft[0:64, :], in1=rz[0:64, :], op=ALU.mult),
                                reads=[ufn, rzn], writes=[on])
                            tcq = (base + q0) // 512
                            sub = (q0 // 256) % 2
                            P.dma("pool", lambda e: e.dma_start(
                                out=oT[2 * g * 64:(2 * g + 2) * 64, base + q0:base + q0 + 256].rearrange(
                                    "(h d) q -> d h q", h=2),
                                in_=ot[:].rearrange("d (h q) -> d h q", h=2)),
                                reads=[on], writes=[("oT", 2 * g, tcq, sub), ("oT", 2 * g + 1, tcq, sub)],
                                key="s_" + on)
                        pipe.defer(3, fin2)
                    pipe.defer(3, fin1)

            for (base, T) in SLOTS:
                for g in range(8):
                    mixer = 0 if g < 2 else (1 if g < 5 else 2)
                    attend(base, T, g, (2 * g, 2 * g + 1), mixer)
            pipe.flush(everything=True)

            for tc in range(NTOK // 512):
                t0 = tc * 512
                on_, ot_ = xa.next()
                oreads = []
                for h in range(16):
                    oreads += [("oT", h, tc, 0), ("oT", h, tc, 1)]
                P.dma("sp", lambda e, ot_=ot_, t0=t0: e.dma_start(
                    out=ot_[:], in_=oT[:, t0:t0 + 512].rearrange("(k p) t -> p k t", p=128)),
                    reads=oreads, writes=[(on_, k) for k in range(8)], key="l_" + on_)
                rs = rms_rstd(on_, ot_, 8, [[0, 1], [2, 3, 4], [5, 6, 7]], [1.0 / 256, 1.0 / 384, 1.0 / 384])
                hn, ht = hTr.next()
                grp_of = [0, 0, 1, 1, 1, 2, 2, 2]
                for k in range(8):
                    rn, rt = rs[grp_of[k]]
                    P.op("dve", lambda e, k=k, ot_=ot_, rt=rt, l=l, ht=ht: e.scalar_tensor_tensor(
                        out=ht[:, k, :], in0=ot_[:, k, :], scalar=g_out(l, k), in1=rt[:],
                        op0=ALU.mult, op1=ALU.mult),
                        reads=[(on_, k), rn, "gains"], writes=[(hn, k)])
                xn, xt = xa.next()
                P.dma("sp", lambda e, xt=xt, t0=t0, src=src: e.dma_start(
                    out=xt[:], in_=src[:, t0:t0 + 512].rearrange("(k p) t -> p k t", p=128)),
                    reads=[(src_key, tc)], writes=[(xn, k) for k in range(8)], key="l_" + xn)
                for j in range(8):
                    wn, wt = load_w(wsl, wb_out[l, j], [("wb_out", l, j)], 1024)
                    pn, pt = psA.next()
                    for k in range(8):
                        P.op("pe", lambda e, k=k, wt=wt, pt=pt, ht=ht: e.matmul(
                            pt[:], wt[:, k * 128:(k + 1) * 128], ht[:, k, :], start=(k == 0), stop=(k == 7)),
                            reads=[wn, (hn, k)], writes=[pn])
                    P.op("dve", lambda e, j=j, xt=xt, pt=pt: e.tensor_tensor(
                        out=xt[:, j, :], in0=xt[:, j, :], in1=pt[:], op=ALU.add),
                        reads=[(xn, j), pn], writes=[(xn, j)])
                (rn, rt), = rms_rstd(xn, xt, 8, [list(range(8))], [1.0 / D])
                hn, ht = hTr.next()
                for k in range(8):
                    P.op("dve", lambda e, k=k, xt=xt, rt=rt, l=l, ht=ht: e.scalar_tensor_tensor(
                        out=ht[:, k, :], in0=xt[:, k, :], scalar=g_ffn(l, k), in1=rt[:],
                        op0=ALU.mult, op1=ALU.mult),
                        reads=[(xn, k), rn, "gains"], writes=[(hn, k)])
                for j in range(NJ):
                    wn, wt = load_w(wsl, wb_gu[l, j], [("wb_gu", l, j)], 2048)
                    gn, gt = psA.next()
                    upn, upt = psB.next()
                    for k in range(8):
                        P.op("pe", lambda e, k=k, wt=wt, gt=gt, ht=ht: e.matmul(
                            gt[:], wt[:, k * 256:k * 256 + 128], ht[:, k, :], start=(k == 0), stop=(k == 7)),
                            reads=[wn, (hn, k)], writes=[gn])
                    for k in range(8):
                        P.op("pe", lambda e, k=k, wt=wt, upt=upt, ht=ht: e.matmul(
                            upt[:], wt[:, k * 256 + 128:k * 256 + 256], ht[:, k, :], start=(k == 0), stop=(k == 7)),
                            reads=[wn, (hn, k)], writes=[upn])
                    sgn, sg = tf.next()
                    P.op("act", lambda e, sg=sg, gt=gt: e.activation(out=sg[:], in_=gt[:], func=AF.Silu),
                         reads=[gn], writes=[sgn])
                    P.op("dve", lambda e, j=j, sg=sg, upt=upt: e.tensor_tensor(
                        out=actT[:, j, :], in0=sg[:], in1=upt[:], op=ALU.mult),
                        reads=[sgn, upn], writes=[("act", j)])
                for j in range(8):
                    wn, wt = load_w(wdn, wb_dn[l, j], [("wb_dn", l, j)], NJ * 128)
                    pn, pt = psA.next()
                    for k in range(NJ):
                        P.op("pe", lambda e, k=k, wt=wt, pt=pt: e.matmul(
                            pt[:], wt[:, k * 128:(k + 1) * 128], actT[:, k, :], start=(k == 0), stop=(k == NJ - 1)),
                            reads=[wn, ("act", k)], writes=[pn])
                    P.op("dve", lambda e, j=j, xt=xt, pt=pt: e.tensor_tensor(
                        out=xt[:, j, :], in0=xt[:, j, :], in1=pt[:], op=ALU.add),
                        reads=[(xn, j), pn], writes=[(xn, j)])
                P.dma("pool", lambda e, xt=xt, t0=t0, dst=dst: e.dma_start(
                    out=dst[:, t0:t0 + 512].rearrange("(k p) t -> p k t", p=128), in_=xt[:]),
                    reads=[(xn, k) for k in range(8)], writes=[(dst_key, tc)], key="s_" + xn)

        P.emit(st)
    return nc


def _rope_tab(pos, dim):
    inv = (np.float32(10000.0) ** (-(np.arange(0, dim, 2, dtype=np.float32)) / np.float32(dim))).astype(np.float32)
    ang = pos.astype(np.float32)[:, None] * inv[None, :]
    ang = np.concatenate([ang, ang], axis=-1)
    return np.cos(ang).astype(np.float32), np.sin(ang).astype(np.float32)


def _rope_input(is_prompt_core):
    posP = np.arange(TP) if is_prompt_core else (np.arange(TP) % TS)
    pos = np.concatenate([posP, np.arange(TS)]).astype(np.int64)
    cr, sr = _rope_tab(pos // 64, 32)
    cc, sc = _rope_tab(pos % 64, 32)
    c1, s1 = _rope_tab(pos, 64)
    sgn32 = np.where(np.arange(32) < 16, -1.0, 1.0).astype(np.float32)
    sgn64 = np.where(np.arange(64) < 32, -1.0, 1.0).astype(np.float32)
    cosA = np.concatenate([cr, cc], axis=1)
    sinA = np.concatenate([sr * sgn32, sc * sgn32], axis=1)
    cosC = c1
    sinC = s1 * sgn64
    out = np.zeros((128, 6, NTOK), np.float32)
    for hh in range(2):
        out[hh * 64:(hh + 1) * 64, 0] = cosA.T
        out[hh * 64:(hh + 1) * 64, 1] = sinA.T
        out[hh * 64:(hh + 1) * 64, 2] = cosC.T
        out[hh * 64:(hh + 1) * 64, 3] = sinC.T
    out[0:64, 4] = 1.0
    out[0:64, 5] = 0.0
    out[64:128, 4] = cosC.T
    out[64:128, 5] = sinC.T
    return out


def _rperm_input():
    R = np.zeros((128, 3, 128), np.float32)
    for dst in range(128):
        d = dst % 64
        base = dst - d
        hb = (d // 32) * 32
        i = d % 32
        ip = i + 16 if i < 16 else i - 16
        R[base + hb + ip, 0, dst] = 1.0
        dp = d + 32 if d < 32 else d - 32
        R[base + dp, 1, dst] = 1.0
        if dst >= 64:
            R[base + dp, 2, dst] = 1.0
    return R.reshape(128, 384).astype(ml_dtypes.bfloat16)


def _bbank_input(rpb):
    kc = np.arange(64)[:, None]
    c = np.arange(64)[None, :]
    c0 = np.clip(c - 8, 0, 48)
    colvalid = (kc >= c0) & (kc < c0 + 16)
    dc = np.clip(kc - c + 15, 0, 30)
    NEG = np.float32(-1e30)
    out = np.full((DEPTH, 6, 128, 2, 15, 64), NEG, np.float32)
    for dr in range(15):
        Tb = np.where(colvalid[None, None], rpb[:, :, dr][:, :, dc], NEG)
        i0 = 14 - dr
        i1 = 15 - dr
        out[:, :, 0:64, 1, i0, :] = Tb
        if i1 <= 14:
            out[:, :, 64:128, 1, i1, :] = Tb
        if 3 <= dr <= 10:
            out[:, :, 0:64, 0, i0, :] = Tb
            out[:, :, 64:128, 0, i1, :] = Tb
    return out.reshape(DEPTH, 6, 128, 2 * 960)


def _bmask_tile(R, seq_rows, j, t):
    m = np.zeros((2, 64, 4, 64), np.float32)
    kt = 2 * j - 2 + t
    if 0 <= kt < R // 2:
        for qr in range(4):
            r = 4 * j + qr
            s = r // seq_rows
            rl = r - s * seq_rows
            r0 = min(max(rl - 4, 0), seq_rows - 8) + s * seq_rows
            for kr in range(2):
                key_r = 2 * kt + kr
                if r0 <= key_r < r0 + 8:
                    m[kr, :, qr, :] = 1.0
    return m.reshape(128, 256)


def _bmask_input(is_prompt_core):
    out = np.zeros((128, 36, 256), np.float32)
    idx = 0
    for j in B_SPECIAL[TP]:
        for t in range(6):
            out[:, idx] = _bmask_tile(64, 64 if is_prompt_core else 32, j, t)
            idx += 1
    for j in B_SPECIAL[TS]:
        for t in range(6):
            out[:, idx] = _bmask_tile(32, 32, j, t)
            idx += 1
    return out.reshape(128, 36 * 256).astype(ml_dtypes.bfloat16)


def _cmask_input():
    i = np.arange(128)[:, None]
    jq = np.arange(256)[None, :]
    out = np.zeros((128, 18, 2, 256), np.float32)
    for o in range(-8, 10):
        d = (jq - i) - 128 * o
        ad = np.abs(d)
        cnt = (ad <= 64).astype(np.float32) + ((d % 4 == 0) & (ad <= 256)) + ((d % 16 == 0) & (ad <= 1024))
        out[:, o + 8, 0] = cnt
        out[:, o + 8, 1] = cnt
    return out.reshape(128, 18 * 512).astype(ml_dtypes.bfloat16)


def _gains_input(norm_mix, norm_ffn, out_gain, q_gain, k_gain):
    g = np.zeros((128, 144), np.float32)
    for l in range(DEPTH):
        for k in range(8):
            g[:, l * 8 + k] = norm_mix[l, k * 128:(k + 1) * 128]
            g[:, 32 + l * 8 + k] = norm_ffn[l, k * 128:(k + 1) * 128]
            g[:, 64 + l * 8 + k] = out_gain[l, k * 128:(k + 1) * 128]
        for c in range(12):
            for hh in range(2):
                if c < 8:
                    head = 2 * c + hh
                    mix = 0 if head < 4 else (1 if head < 10 else 2)
                    g[hh * 64:(hh + 1) * 64, 96 + l * 12 + c] = q_gain[l, mix]
                else:
                    kv = 2 * (c - 8) + hh
                    mix = 0 if kv < 2 else (1 if kv < 5 else 2)
                    g[hh * 64:(hh + 1) * 64, 96 + l * 12 + c] = k_gain[l, mix]
    return g


_NC_CACHE = {}


def kernel(x_prompt, x_sample, norm_mix, w_in, q_gain, k_gain, rpb, out_gain, w_out, norm_ffn, w_gate_up, w_down):
    f = lambda a: np.ascontiguousarray(np.asarray(a, dtype=np.float32))
    x_prompt, x_sample = f(x_prompt), f(x_sample)
    w_in, w_out, w_gate_up, w_down = f(w_in), f(w_out), f(w_gate_up), f(w_down)
    rpb = f(rpb)
    w_in_t = np.ascontiguousarray(
        w_in[:, :, :1536].reshape(DEPTH, 8, 128, 12, 128).transpose(0, 3, 2, 1, 4)).reshape(DEPTH, 12, 128, 1024)
    w_v_t = np.ascontiguousarray(
        w_in[:, :, 1536:].reshape(DEPTH, 8, 128, 2, 256).transpose(0, 3, 2, 1, 4)).reshape(DEPTH, 2, 128, 2048)
    w_out_t = np.ascontiguousarray(
        w_out.reshape(DEPTH, 8, 128, 8, 128).transpose(0, 3, 2, 1, 4)).reshape(DEPTH, 8, 128, 1024)
    gu = w_gate_up.reshape(DEPTH, 8, 128, 2, NJ, 128)
    w_gu_t = np.ascontiguousarray(gu.transpose(0, 4, 2, 1, 3, 5)).reshape(DEPTH, NJ, 128, 2048)
    w_dn_t = np.ascontiguousarray(
        w_down.reshape(DEPTH, NJ, 128, 8, 128).transpose(0, 3, 2, 1, 4)).reshape(DEPTH, 8, 128, NJ * 128)
    gains = _gains_input(f(norm_mix), f(norm_ffn), f(out_gain), f(q_gain), f(k_gain))
    bbank = _bbank_input(rpb)
    rperm = _rperm_input()
    cmask = _cmask_input()
    shared = dict(w_in_t=w_in_t, w_v_t=w_v_t, w_out_t=w_out_t, w_gu_t=w_gu_t, w_dn_t=w_dn_t, gains=gains,
                  bbank=bbank, rperm=rperm, cmask=cmask)
    per_type = {}
    for ip in (True, False):
        per_type[ip] = dict(
            rope=_rope_input(ip), bmask=_bmask_input(ip),
            cross=np.full((128, 1), 0.0 if ip else -30000.0, np.float32))
    in_maps = []
    for c in range(8):
        if c < 4:
            toks = np.concatenate([x_prompt[c], x_sample[c]], axis=0)
        else:
            i = c - 4
            toks = np.concatenate([x_sample[8 + 2 * i], x_sample[9 + 2 * i], x_sample[4 + i]], axis=0)
        m = dict(shared)
        m.update(per_type[c < 4])
        m["xT"] = np.ascontiguousarray(toks.T)
        in_maps.append(m)
    if "nc" not in _NC_CACHE:
        _NC_CACHE["nc"] = build_nc()
    res = run_bass_kernel_spmd(_NC_CACHE["nc"], in_maps, core_ids=list(range(8)))
    y_prompt = np.zeros_like(x_prompt)
    y_sample = np.zeros_like(x_sample)
    for c in range(8):
        y = np.ascontiguousarray(res.results[c]["yT"].T)
        if c < 4:
            y_prompt[c] = y[:TP]
            y_sample[c] = y[TP:]
        else:
            i = c - 4
            y_sample[8 + 2 * i] = y[:TS]
            y_sample[9 + 2 * i] = y[TS:TP]
            y_sample[4 + i] = y[TP:]
    return (y_prompt, y_sample)
```

```python
import contextlib
import numpy as np
import ml_dtypes
import concourse.bass as bass
import concourse.mybir as mybir
from concourse.bass_utils import run_bass_kernel_spmd

F32 = mybir.dt.float32
BF = mybir.dt.bfloat16
AF = mybir.ActivationFunctionType
ALU = mybir.AluOpType

D = 1024
DEPTH = 4
NTOK = 6144
TP, TS = 4096, 2048
FFN = 2816
NJ = FFN // 128
EPS = 1e-6
SCALE = 0.125
SLOTS = ((0, TP), (TP, TS))
CHUNK_TYPE = {0: 0, 1: 0, 5: 1, 6: 1, 7: 1, 8: 0, 10: 2, 11: 1}
B_SPECIAL = {TP: (0, 7, 8, 15), TS: (0, 7)}
N_LAYERS_BUILD = DEPTH


class Prog:
    COMPUTE = ("pe", "act", "dve", "pool")
    EPOCH = 20000

    def __init__(self, nc):
        self.nc = nc
        self.ops = []
        self.last_w = {}
        self.readers = {}
        self.dma_count = {}

    def _add(self, eng, fn, reads, writes, dma_key=None):
        i = len(self.ops)
        deps = set()
        for b in reads:
            j = self.last_w.get(b)
            if j is not None:
                deps.add(j)
        for b in writes:
            j = self.last_w.get(b)
            if j is not None:
                deps.add(j)
            for j in self.readers.get(b, {}).values():
                deps.add(j)
        val = None
        if dma_key is not None:
            self.dma_count[dma_key] = self.dma_count.get(dma_key, 0) + 1
            val = 16 * self.dma_count[dma_key]
        self.ops.append([eng, fn, deps, dma_key, val, False])
        for b in writes:
            self.last_w[b] = i
            self.readers[b] = {}
        rk = eng if dma_key is None else ("dma", i)
        for b in reads:
            self.readers.setdefault(b, {})[rk] = i
        return i

    def op(self, eng, fn, reads=(), writes=()):
        return self._add(eng, fn, tuple(reads), tuple(writes))

    def dma(self, queue, fn, reads=(), writes=(), key=None):
        return self._add(queue, fn, tuple(reads), tuple(writes), dma_key=key)

    def emit(self, stack):
        nc = self.nc
        ops = self.ops
        for o in ops:
            for j in o[2]:
                pj = ops[j]
                if pj[3] is None:
                    if pj[0] == "pe" and o[0] == "pe" and o[3] is None:
                        continue
                    pj[5] = True
        sems = {}
        cnt = {e: 0 for e in self.COMPUTE}
        sig = {}
        for i, o in enumerate(ops):
            if o[3] is None:
                if o[5]:
                    cnt[o[0]] += 1
                    ep, v = divmod(cnt[o[0]] - 1, self.EPOCH)
                    sig[i] = ((o[0], ep), v + 1)
            else:
                sig[i] = (("dma", o[3]), o[4])
        for sk, _ in sig.values():
            if sk not in sems:
                sems[sk] = stack.enter_context(nc.semaphore("s%d" % len(sems)))
        per_eng = {}
        for i, o in enumerate(ops):
            per_eng.setdefault(o[0], []).append(i)
        engmap = {"pe": "tensor", "act": "scalar", "dve": "vector", "pool": "gpsimd", "sp": "sync"}
        final_waits = [(sems[("dma", k)], 16 * c) for k, c in self.dma_count.items()]
        block = stack.enter_context(nc.Block())

        def make(ename, idxs):
            def body(eng):
                waited = {}
                for i in idxs:
                    o = ops[i]
                    need = {}
                    for j in o[2]:
                        if j not in sig:
                            continue
                        sk, v = sig[j]
                        if v > need.get(sk, 0):
                            need[sk] = v
                    for sk, v in need.items():
                        if waited.get(sk, 0) >= v:
                            continue
                        eng.wait_ge(sems[sk], v)
                        waited[sk] = v
                    ins = o[1](eng)
                    if i in sig:
                        sk, v = sig[i]
                        ins.then_inc(sems[sk], 16 if o[3] is not None else 1)
                if ename == "sp":
                    for s, v in final_waits:
                        eng.wait_ge(s, v)
            return body

        if "sp" not in per_eng:
            per_eng["sp"] = []
        for ename, idxs in per_eng.items():
            getattr(block, engmap[ename])(make(ename, idxs))


class Rot:
    def __init__(self, items):
        self.items = list(items)
        self.i = -1

    def next(self):
        self.i = (self.i + 1) % len(self.items)
        return self.items[self.i]


def build_nc():
    nc = bass.Bass("TRN2", target_bir_lowering=False)

    def din(name, shape, dt=F32):
        return nc.dram_tensor(name, list(shape), dt, kind="ExternalInput").ap()

    def dscr(name, shape, dt):
        return nc.dram_tensor(name, list(shape), dt, kind="Internal").ap()

    xT = din("xT", [D, NTOK])
    yT = nc.dram_tensor("yT", [D, NTOK], F32, kind="ExternalOutput").ap()
    w_in_t = din("w_in_t", [DEPTH, 12, 128, 8 * 128])
    w_v_t = din("w_v_t", [DEPTH, 2, 128, 8 * 256])
    w_out_t = din("w_out_t", [DEPTH, 8, 128, 8 * 128])
    w_gu_t = din("w_gu_t", [DEPTH, NJ, 128, 8 * 256])
    w_dn_t = din("w_dn_t", [DEPTH, 8, 128, NJ * 128])
    gains_d = din("gains", [128, 3 * 32 + 48])
    rope_d = din("rope", [128, 6, NTOK])
    rperm_d = din("rperm", [128, 3 * 128], BF)
    bbank_d = din("bbank", [DEPTH, 6, 128, 2 * 960])
    bmask_d = din("bmask", [128, 36 * 256], BF)
    cmask_d = din("cmask", [128, 18 * 512], BF)
    cross_d = din("cross", [128, 1])

    wb_in = dscr("wb_in", [DEPTH, 12, 128, 8 * 128], BF)
    wb_v = dscr("wb_v", [DEPTH, 2, 128, 8 * 256], BF)
    wb_out = dscr("wb_out", [DEPTH, 8, 128, 8 * 128], BF)
    wb_gu = dscr("wb_gu", [DEPTH, NJ, 128, 8 * 256], BF)
    wb_dn = dscr("wb_dn", [DEPTH, 8, 128, NJ * 128], BF)
    xres = dscr("xres", [D, NTOK], F32)
    qkT = dscr("qkT", [12 * 128, NTOK], BF)
    Vd = dscr("Vd", [NTOK, 512], BF)
    oT = dscr("oT", [D, NTOK], F32)

    st = contextlib.ExitStack()
    with st:
        def sb(name, shape, dt):
            return st.enter_context(nc.sbuf_tensor("sb_" + name, list(shape), dt))

        def psum(name):
            return st.enter_context(nc.psum_tensor(name, [128, 512], F32))

        P = Prog(nc)
        uid = [0]

        def U(prefix):
            uid[0] += 1
            return (prefix, uid[0])

        gains = sb("gains", [128, 144], F32)
        rperm = sb("rperm", [128, 3, 128], BF)
        ones_bf = sb("ones_bf", [128, 128], BF)
        bones = sb("bones", [128, 128], BF)
        sel65 = sb("sel65", [65, 64], F32)
        cmask = sb("cmask", [128, 18, 512], BF)
        bmk = sb("bmk", [128, 6, 256], BF)

        P.dma("sp", lambda e: e.dma_start(out=gains[:], in_=gains_d), writes=["gains"], key="c_gains")
        P.dma("sp", lambda e: e.dma_start(out=rperm[:], in_=rperm_d.rearrange("p (a b) -> p a b", a=3)),
              writes=["rperm"], key="c_rperm")
        P.dma("sp", lambda e: e.dma_start(out=cmask[:], in_=cmask_d.rearrange("p (a b) -> p a b", a=18)),
              writes=["cmask"], key="c_cmask")
        P.op("pool", lambda e: e.memset(ones_bf[:], 1.0), writes=["ones_bf"])
        P.op("pool", lambda e: e.memset(bones[:], 0.0), writes=["bones"])
        P.op("pool", lambda e: e.memset(bones[0:64, 0:64], 1.0), reads=[], writes=["bones"])
        P.op("pool", lambda e: e.memset(bones[64:128, 64:128], 1.0), reads=[], writes=["bones"])
        P.op("pool", lambda e: e.memset(sel65[:], 0.0), writes=["sel65"])
        P.op("pool", lambda e: e.memset(sel65[64:65, :], 1.0), writes=["sel65"])

        def g_mix(l, k):
            return gains[:, l * 8 + k: l * 8 + k + 1]

        def g_ffn(l, k):
            return gains[:, 32 + l * 8 + k: 32 + l * 8 + k + 1]

        def g_out(l, k):
            return gains[:, 64 + l * 8 + k: 64 + l * 8 + k + 1]

        def g_qk(l, c):
            return gains[:, 96 + l * 12 + c: 96 + l * 12 + c + 1]

        def cast_w(src, dst, l, name):
            s2 = src[l]
            d2 = dst[l]
            nd = len(s2.shape)
            if nd == 3:
                s2 = s2.rearrange("a p n -> (a p) n")
                d2 = d2.rearrange("a p n -> (a p) n")
            rows = s2.shape[0]
            step = 512
            for r0 in range(0, rows, step):
                r1 = min(rows, r0 + step)
                P.dma("pool", lambda e, a=d2[r0:r1, :], b=s2[r0:r1, :]: e.dma_start(out=a, in_=b),
                      writes=[(name, l, r0 // 128 + i) for i in range((r1 - r0 + 127) // 128)],
                      key=("cast", name, l, r0))

        for l in range(N_LAYERS_BUILD):
            cast_w(w_in_t, wb_in, l, "wb_in")
            cast_w(w_v_t, wb_v, l, "wb_v")
            cast_w(w_out_t, wb_out, l, "wb_out")
            cast_w(w_gu_t, wb_gu, l, "wb_gu")
            cast_w(w_dn_t, wb_dn, l, "wb_dn")

        bank = [("ps%d" % i, psum("ps%d" % i)) for i in range(8)]
        psA = Rot(bank[0:3])
        psB = Rot(bank[3:5])
        psZ = bank[5]
        psN = bank[6]
        psH = bank[7]
        psS = Rot([bank[0], bank[1], bank[2], bank[6], bank[7]])

        xa = Rot([("xa%d" % i, sb("xa%d" % i, [128, 8, 512], F32)) for i in range(2)])
        hTr = Rot([("hT%d" % i, sb("hT%d" % i, [128, 8, 512], BF)) for i in range(2)])
        ropeR = Rot([("rope%d" % i, sb("rope%d" % i, [128, 6, 512], F32)) for i in range(2)])
        wsl = Rot([("w%d" % i, sb("w%d" % i, [128, 8 * 256], BF)) for i in range(3)])
        wdn = Rot([("wd%d" % i, sb("wd%d" % i, [128, NJ * 128], BF)) for i in range(2)])
        actT = sb("actT", [128, NJ, 512], BF)
        sqb = Rot([("sqb%d" % i, sb("sqb%d" % i, [128, 512], BF)) for i in range(2)])
        tf = Rot([("tf%d" % i, sb("tf%d" % i, [128, 512], F32)) for i in range(4)])
        tb = Rot([("tb%d" % i, sb("tb%d" % i, [128, 512], BF)) for i in range(5)])
        rstd3 = [("rs%d" % i, sb("rs%d" % i, [128, 512], F32)) for i in range(3)]
        sqc = Rot([("sqc%d" % i, sb("sqc%d" % i, [128, 512], BF)) for i in range(2)])
        stq = Rot([("stq%d" % i, sb("stq%d" % i, [128, 512], BF)) for i in range(2)])
        stv = Rot([("stv%d" % i, sb("stv%d" % i, [128, 4, 512], BF)) for i in range(1)])
        KT = Rot([("KT%d" % i, sb("KT%d" % i, [128, TP], BF)) for i in range(1)])
        QT = Rot([("QT%d" % i, sb("QT%d" % i, [128, 256], BF)) for i in range(3)])
        Vt = Rot([("Vt%d" % i, sb("Vt%d" % i, [128, 32, 128], BF)) for i in range(1)])
        ebR = Rot([("eb%d" % i, sb("eb%d" % i, [128, 2, 960], F32)) for i in range(2)])
        uf = Rot([("uf%d" % i, sb("uf%d" % i, [65, 512], F32)) for i in range(2)])
        ost = Rot([("ost%d" % i, sb("ost%d" % i, [64, 512], F32)) for i in range(1)])

        crossb = sb("crossb", [128, 1], F32)
        P.dma("sp", lambda e: e.dma_start(out=crossb[:], in_=cross_d), writes=["crossb"], key="c_crossb")
        for (vn, vt_) in Vt.items:
            P.op("pool", lambda e, t=vt_: e.memset(t[:, :, 64:128], 0.0), writes=[vn])
            P.op("pool", lambda e, t=vt_: e.memset(t[:, :, 64:65], 1.0), writes=[vn])

        class Pipe:
            def __init__(self):
                self.step = 0
                self.q = []
                self.seq = 0

            def defer(self, lag, fn):
                self.q.append((self.step + lag, self.seq, fn))
                self.seq += 1

            def tick(self):
                self.step += 1
                self.flush()

            def flush(self, everything=False):
                while True:
                    ready = [x for x in self.q if everything or x[0] <= self.step]
                    if not ready:
                        break
                    ready.sort()
                    x = ready[0]
                    self.q.remove(x)
                    x[2]()

        pipe = Pipe()

        def load_w(rot, src2d, deps, ncols):
            name, t = rot.next()
            P.dma("sp", lambda e, t=t, s=src2d, n=ncols: e.dma_start(out=t[:, 0:n], in_=s),
                  reads=deps, writes=[name], key="l_" + name)
            return name, t

        def recip(out_ap, in_ap, reads, writes, npart=128, n=512):
            P.op("dve", lambda e: e.reciprocal(out=out_ap, in_=in_ap), reads=list(reads), writes=list(writes))

        def rms_rstd(src_name, src_t, nk, groups, inv_dims):
            outs = []
            for gi, grp in enumerate(groups):
                pn, pt = psN
                for ii, k in enumerate(grp):
                    sn, s_ = sqb.next()
                    P.op("act", lambda e, s_=s_, k=k: e.activation(out=s_[:], in_=src_t[:, k, :], func=AF.Square),
                         reads=[(src_name, k)], writes=[sn])
                    P.op("pe", lambda e, s_=s_, a=(ii == 0), b=(ii == len(grp) - 1), pt=pt:
                         e.matmul(pt[:], ones_bf[:], s_[:], start=a, stop=b),
                         reads=[sn, "ones_bf"], writes=[pn])
                rn, rt = rstd3[gi]
                tn, tt_ = tf.next()
                P.op("act", lambda e, tt_=tt_, pt=pt, sc=inv_dims[gi]:
                     e.activation(out=tt_[:], in_=pt[:], func=AF.Sqrt, scale=sc, bias=EPS),
                     reads=[pn], writes=[tn])
                recip(rt[:], tt_[:], [tn], [rn])
                outs.append((rn, rt))
            return outs

        for l in range(N_LAYERS_BUILD):
            src = xT if l == 0 else xres
            dst = yT if l == N_LAYERS_BUILD - 1 else xres
            src_key = "xin" if l == 0 else "xres"
            dst_key = "yT" if l == N_LAYERS_BUILD - 1 else "xres"

            def prep(tc, l=l, src=src, src_key=src_key):
                t0 = tc * 512
                xn, xt = xa.next()
                P.dma("sp", lambda e, xt=xt, t0=t0, src=src: e.dma_start(
                    out=xt[:], in_=src[:, t0:t0 + 512].rearrange("(k p) t -> p k t", p=128)),
                    reads=[(src_key, tc)], writes=[(xn, k) for k in range(8)], key="l_" + xn)
                rpn, rpt = ropeR.next()
                P.dma("sp", lambda e, t0=t0, rpt=rpt: e.dma_start(out=rpt[:], in_=rope_d[:, :, t0:t0 + 512]),
                      writes=[rpn], key="l_" + rpn)
                (rn, rt), = rms_rstd(xn, xt, 8, [list(range(8))], [1.0 / D])
                hn, ht = hTr.next()
                for k in range(8):
                    P.op("dve", lambda e, k=k, xt=xt, rt=rt, l=l, ht=ht: e.scalar_tensor_tensor(
                        out=ht[:, k, :], in0=xt[:, k, :], scalar=g_mix(l, k), in1=rt[:],
                        op0=ALU.mult, op1=ALU.mult),
                        reads=[(xn, k), rn, "gains"], writes=[(hn, k)])
                return hn, ht, rpn, rpt

            def qk_chunk(tc, c, hn, ht, rpn, rpt, l=l):
                t0 = tc * 512
                wn, wt = load_w(wsl, wb_in[l, c], [("wb_in", l, c)], 1024)
                pn, pt = psA.next()
                for k in range(8):
                    P.op("pe", lambda e, k=k, wt=wt, pt=pt: e.matmul(
                        pt[:], wt[:, k * 128:(k + 1) * 128], ht[:, k, :], start=(k == 0), stop=(k == 7)),
                        reads=[wn, (hn, k)], writes=[pn])
                sn, s_ = sqc.next()
                P.op("act", lambda e, s_=s_, pt=pt: e.activation(out=s_[:], in_=pt[:], func=AF.Square),
                     reads=[pn], writes=[sn])
                ty = CHUNK_TYPE.get(c)
                state = {}

                def part2():
                    hhn, hht = psH
                    P.op("pe", lambda e, s_=s_, hht=hht: e.matmul(hht[:], bones[:], s_[:], start=True, stop=True),
                         reads=[sn, "bones"], writes=[hhn])
                    t1n, t1 = tf.next()
                    P.op("act", lambda e, t1=t1, hht=hht: e.activation(out=t1[:], in_=hht[:], func=AF.Sqrt,
                                                                       scale=1.0 / 64, bias=EPS),
                         reads=[hhn], writes=[t1n])
                    r2n, r2 = tf.next()
                    recip(r2[:], t1[:], [t1n], [r2n])
                    qsn, qs = stq.next()
                    state["qs"] = (qsn, qs)
                    if ty is None:
                        P.op("dve", lambda e, qs=qs, r2=r2: e.scalar_tensor_tensor(
                            out=qs[:], in0=pt[:], scalar=g_qk(l, c), in1=r2[:], op0=ALU.mult, op1=ALU.mult),
                            reads=[pn, r2n, "gains"], writes=[qsn])
                        store(qsn, qs)
                    else:
                        qfn, qf = tf.next()
                        P.op("dve", lambda e, qf=qf, r2=r2: e.scalar_tensor_tensor(
                            out=qf[:], in0=pt[:], scalar=g_qk(l, c), in1=r2[:], op0=ALU.mult, op1=ALU.mult),
                            reads=[pn, r2n, "gains"], writes=[qfn])
                        qbn, qb_ = tb.next()
                        P.op("pool", lambda e, qb_=qb_, qf=qf: e.tensor_copy(out=qb_[:], in_=qf[:]),
                             reads=[qfn], writes=[qbn])
                        an, a_ = tf.next()
                        P.op("pool", lambda e, a_=a_, qf=qf: e.tensor_tensor(
                            out=a_[:], in0=qf[:], in1=rpt[:, 2 * ty, :], op=ALU.mult),
                            reads=[qfn, rpn], writes=[an])
                        state["rot"] = (qbn, qb_, an, a_)

                def store(qsn, qs):
                    P.dma("pool", lambda e, qs=qs: e.dma_start(
                        out=qkT[c * 128:(c + 1) * 128, t0:t0 + 512], in_=qs[:]),
                        reads=[qsn], writes=[("qkT", c, tc)], key="s_" + qsn)

                def part3():
                    qbn, qb_, an, a_ = state["rot"]
                    qsn, qs = state["qs"]
                    rbn, rb = psB.next()
                    P.op("pe", lambda e, rb=rb, qb_=qb_: e.matmul(rb[:], rperm[:, ty, :], qb_[:], start=True, stop=True),
                         reads=[qbn, "rperm"], writes=[rbn])
                    bn, b_ = tf.next()
                    P.op("dve", lambda e, b_=b_, rb=rb: e.tensor_tensor(
                        out=b_[:], in0=rb[:], in1=rpt[:, 2 * ty + 1, :], op=ALU.mult),
                        reads=[rbn, rpn], writes=[bn])
                    P.op("pool", lambda e, qs=qs, a_=a_, b_=b_: e.tensor_tensor(
                        out=qs[:], in0=a_[:], in1=b_[:], op=ALU.add),
                        reads=[an, bn], writes=[qsn])
                    store(qsn, qs)

                pipe.defer(2, part2)
                if ty is not None:
                    pipe.defer(3, part3)
                pipe.tick()

            def v_proj(tc, hn, ht, l=l):
                t0 = tc * 512
                svn, sv = stv.next()
                for half in range(2):
                    wn, wt = load_w(wsl, wb_v[l, half], [("wb_v", l, half)], 2048)
                    for tt in range(4):
                        pn, pt = psA.next()
                        for k in range(8):
                            P.op("pe", lambda e, k=k, tt=tt, pt=pt, wt=wt: e.matmul(
                                pt[:, 0:256], ht[:, k, tt * 128:(tt + 1) * 128], wt[:, k * 256:(k + 1) * 256],
                                start=(k == 0), stop=(k == 7)),
                                reads=[wn, (hn, k)], writes=[pn])
                        P.op("act", lambda e, sv=sv, tt=tt, pt=pt, half=half: e.activation(
                            out=sv[:, tt, half * 256:(half + 1) * 256], in_=pt[:, 0:256], func=AF.Copy),
                            reads=[pn], writes=[(svn, tt, half)])
                        pipe.tick()
                P.dma("pool", lambda e, sv=sv: e.dma_start(
                    out=Vd[t0:t0 + 512, :].rearrange("(tt p) n -> p tt n", p=128), in_=sv[:]),
                    reads=[(svn, tt, hf) for tt in range(4) for hf in range(2)], writes=[("Vd", tc)], key="s_" + svn)

            ntc = NTOK // 512
            cur = prep(0)
            for tc in range(ntc):
                hn, ht, rpn, rpt = cur
                for c in range(12):
                    qk_chunk(tc, c, hn, ht, rpn, rpt)
                    if c == 5 and tc + 1 < ntc:
                        cur = prep(tc + 1)
                v_proj(tc, hn, ht)
            pipe.flush(everything=True)

            def attend(base, T, g, heads, mixer, l=l):
                ntile = T // 128
                tcs = range(base // 512, (base + T) // 512)
                pipe.flush(everything=True)
                kn, kt_ = KT.next()
                for hh in range(2):
                    P.dma("sp", lambda e, kt_=kt_, hh=hh: e.dma_start(
                        out=kt_[hh * 64:(hh + 1) * 64, 0:T],
                        in_=qkT[1024 + g * 64:1024 + (g + 1) * 64, base:base + T]),
                        reads=[("qkT", 8 + g // 2, tc) for tc in tcs], writes=[kn], key="l_" + kn)
                vn, vt_ = Vt.next()
                P.dma("sp", lambda e, vt_=vt_: e.dma_start(
                    out=vt_[:, 0:ntile, 0:64],
                    in_=Vd[base:base + T, g * 64:(g + 1) * 64].rearrange("(n p) d -> p n d", p=128)),
                    reads=[("Vd", tc) for tc in tcs], writes=[vn], key="l_" + vn)
                ebs = []
                if mixer == 1:
                    for h in heads:
                        ebn, ebt = ebR.next()
                        P.dma("sp", lambda e, h=h, ebt=ebt: e.dma_start(
                            out=ebt[:], in_=bbank_d[l, h - 4].rearrange("p (a b) -> p a b", a=2)),
                            writes=[ebn], key="l_" + ebn)
                        P.op("act", lambda e, ebt=ebt: e.activation(out=ebt[:], in_=ebt[:], func=AF.Exp),
                             reads=[ebn], writes=[ebn])
                        ebs.append((ebn, ebt))
                N = 256
                for qb in range(T // N):
                    q0 = qb * N
                    qn, qt_ = QT.next()
                    P.dma("sp", lambda e, qt_=qt_, q0=q0: e.dma_start(
                        out=qt_[:, 0:N], in_=qkT[g * 128:(g + 1) * 128, base + q0:base + q0 + N]),
                        reads=[("qkT", g, (base + q0) // 512)], writes=[qn], key="l_" + qn)
                    tiles = []
                    if mixer == 0:
                        for kt in range(ntile):
                            cross = (T == TP) and ((q0 < TS) != (kt < 16))
                            tiles.append((kt, cross, None, None))
                    elif mixer == 2:
                        for kt in range(ntile):
                            o = kt - 2 * qb
                            if -8 <= o <= 9:
                                cross = (T == TP) and ((q0 < TS) != (kt < 16))
                                tiles.append((kt, cross, ("c", o + 8), None))
                    else:
                        spec = B_SPECIAL[T]
                        is_spec = qb in spec
                        if is_spec:
                            mi = (0 if T == TP else 4) + spec.index(qb)
                            P.dma("sp", lambda e, mi=mi: e.dma_start(
                                out=bmk[:], in_=bmask_d[:, mi * 6 * 256:(mi + 1) * 6 * 256].rearrange(
                                    "p (a b) -> p a b", a=6)),
                                writes=["bmk"], key="l_bmk")
                        for t in range(6):
                            kt = 2 * qb - 2 + t
                            if 0 <= kt < ntile:
                                tiles.append((kt, False, ("b", 1 if is_spec else 0, t), t if is_spec else None))
                    un, ut = psB.next()
                    for ti, (kt, cross, m1, m2) in enumerate(tiles):
                        sc = [psS.next(), psS.next()]
                        for hh in range(2):
                            P.op("pe", lambda e, s_=sc[hh][1], kt=kt, qt_=qt_, hh=hh: e.matmul(
                                s_[:, 0:256],
                                kt_[hh * 64:(hh + 1) * 64, kt * 128:(kt + 1) * 128],
                                qt_[hh * 64:(hh + 1) * 64, 0:256],
                                start=True, stop=True),
                                reads=[kn, qn], writes=[sc[hh][0]])
                        pbn, pb = tb.next()
                        if m1 is None:
                            dstn, dstt = pbn, pb
                        else:
                            dstn, dstt = tf.next()
                        for hh in range(2):
                            if cross:
                                P.op("act", lambda e, dstt=dstt, s_=sc[hh][1], hh=hh: e.activation(
                                    out=dstt[:, hh * 256:(hh + 1) * 256], in_=s_[:, 0:256], func=AF.Exp,
                                    scale=SCALE, bias=crossb[:, 0:1]),
                                    reads=[sc[hh][0], "crossb"], writes=[dstn])
                            else:
                                P.op("act", lambda e, dstt=dstt, s_=sc[hh][1], hh=hh: e.activation(
                                    out=dstt[:, hh * 256:(hh + 1) * 256], in_=s_[:, 0:256], func=AF.Exp,
                                    scale=SCALE),
                                    reads=[sc[hh][0]], writes=[dstn])
                        if m1 is not None:
                            ef = dstt
                            efn = dstn
                            if m1[0] == "c":
                                P.op("dve", lambda e, pb=pb, ef=ef, oi=m1[1]: e.tensor_tensor(
                                    out=pb[:], in0=ef[:], in1=cmask[:, oi, :], op=ALU.mult),
                                    reads=[efn, "cmask"], writes=[pbn])
                            else:
                                v, t = m1[1], m1[2]
                                i0 = (11 - 2 * t) * 64
                                if m2 is None:
                                    for hh in range(2):
                                        ebn, ebt = ebs[hh]
                                        P.op("pool" if hh == 0 else "dve", lambda e, pb=pb, ef=ef, v=v, i0=i0, hh=hh, ebt=ebt: e.tensor_tensor(
                                            out=pb[:, hh * 256:(hh + 1) * 256], in0=ef[:, hh * 256:(hh + 1) * 256],
                                            in1=ebt[:, v, i0:i0 + 256], op=ALU.mult),
                                            reads=[efn, ebn], writes=[(pbn, "h", hh)])
                                else:
                                    e2n, e2 = tf.next()
                                    for hh in range(2):
                                        ebn, ebt = ebs[hh]
                                        P.op("dve", lambda e, e2=e2, ef=ef, v=v, i0=i0, hh=hh, ebt=ebt: e.tensor_tensor(
                                            out=e2[:, hh * 256:(hh + 1) * 256], in0=ef[:, hh * 256:(hh + 1) * 256],
                                            in1=ebt[:, v, i0:i0 + 256], op=ALU.mult),
                                            reads=[efn, ebn], writes=[e2n])
                                    for hh in range(2):
                                        P.op("pool", lambda e, pb=pb, e2=e2, t=m2, hh=hh: e.tensor_tensor(
                                            out=pb[:, hh * 256:(hh + 1) * 256], in0=e2[:, hh * 256:(hh + 1) * 256],
                                            in1=bmk[:, t, :], op=ALU.mult),
                                            reads=[e2n, "bmk"], writes=[pbn])

                        def pv(ut=ut, un=un, kt=kt, pb=pb, pbn=pbn, a=(ti == 0), b=(ti == len(tiles) - 1)):
                            P.op("pe", lambda e: e.matmul(
                                ut[:, :], vt_[:, kt, :], pb[:], start=a, stop=b),
                                reads=[vn, pbn, (pbn, "h", 0), (pbn, "h", 1)], writes=[un])
                        pipe.defer(4, pv)
                        pipe.tick()

                    def fin1(ut=ut, un=un, q0=q0):
                        ufn, uft = uf.next()
                        P.op("act", lambda e: e.activation(out=uft[:], in_=ut[0:65, :], func=AF.Copy),
                             reads=[un], writes=[ufn])

                        def fin2():
                            zn, zt = psZ
                            P.op("pe", lambda e: e.matmul(zt[0:64, :], sel65[:], uft[:], start=True, stop=True),
                                 reads=[ufn, "sel65"], writes=[zn])
                            rzn, rz = tf.next()
                            recip(rz[0:64, :], zt[0:64, :], [zn], [rzn], npart=64, n=512)
                            on, ot = ost.next()
                            P.op("pool", lambda e: e.tensor_tensor(
                                out=ot[:], in0=uft[0:64, :], in1=rz[0:64, :], op=ALU.mult),
                                reads=[ufn, rzn], writes=[on])
                            tcq = (base + q0) // 512
                            sub = (q0 // 256) % 2
                            P.dma("pool", lambda e: e.dma_start(
                                out=oT[2 * g * 64:(2 * g + 2) * 64, base + q0:base + q0 + 256].rearrange(
                                    "(h d) q -> d h q", h=2),
                                in_=ot[:].rearrange("d (h q) -> d h q", h=2)),
                                reads=[on], writes=[("oT", 2 * g, tcq, sub), ("oT", 2 * g + 1, tcq, sub)],
                                key="s_" + on)
                        pipe.defer(3, fin2)
                    pipe.defer(3, fin1)

            for (base, T) in SLOTS:
                for g in range(8):
                    mixer = 0 if g < 2 else (1 if g < 5 else 2)
                    attend(base, T, g, (2 * g, 2 * g + 1), mixer)
            pipe.flush(everything=True)

            for tc in range(NTOK // 512):
                t0 = tc * 512
                on_, ot_ = xa.next()
                oreads = []
                for h in range(16):
                    oreads += [("oT", h, tc, 0), ("oT", h, tc, 1)]
                P.dma("sp", lambda e, ot_=ot_, t0=t0: e.dma_start(
                    out=ot_[:], in_=oT[:, t0:t0 + 512].rearrange("(k p) t -> p k t", p=128)),
                    reads=oreads, writes=[(on_, k) for k in range(8)], key="l_" + on_)
                rs = rms_rstd(on_, ot_, 8, [[0, 1], [2, 3, 4], [5, 6, 7]], [1.0 / 256, 1.0 / 384, 1.0 / 384])
                hn, ht = hTr.next()
                grp_of = [0, 0, 1, 1, 1, 2, 2, 2]
                for k in range(8):
                    rn, rt = rs[grp_of[k]]
                    P.op("dve", lambda e, k=k, ot_=ot_, rt=rt, l=l, ht=ht: e.scalar_tensor_tensor(
                        out=ht[:, k, :], in0=ot_[:, k, :], scalar=g_out(l, k), in1=rt[:],
                        op0=ALU.mult, op1=ALU.mult),
                        reads=[(on_, k), rn, "gains"], writes=[(hn, k)])
                xn, xt = xa.next()
                P.dma("sp", lambda e, xt=xt, t0=t0, src=src: e.dma_start(
                    out=xt[:], in_=src[:, t0:t0 + 512].rearrange("(k p) t -> p k t", p=128)),
                    reads=[(src_key, tc)], writes=[(xn, k) for k in range(8)], key="l_" + xn)
                for j in range(8):
                    wn, wt = load_w(wsl, wb_out[l, j], [("wb_out", l, j)], 1024)
                    pn, pt = psA.next()
                    for k in range(8):
                        P.op("pe", lambda e, k=k, wt=wt, pt=pt, ht=ht: e.matmul(
                            pt[:], wt[:, k * 128:(k + 1) * 128], ht[:, k, :], start=(k == 0), stop=(k == 7)),
                            reads=[wn, (hn, k)], writes=[pn])
                    P.op("dve", lambda e, j=j, xt=xt, pt=pt: e.tensor_tensor(
                        out=xt[:, j, :], in0=xt[:, j, :], in1=pt[:], op=ALU.add),
                        reads=[(xn, j), pn], writes=[(xn, j)])
                (rn, rt), = rms_rstd(xn, xt, 8, [list(range(8))], [1.0 / D])
                hn, ht = hTr.next()
                for k in range(8):
                    P.op("dve", lambda e, k=k, xt=xt, rt=rt, l=l, ht=ht: e.scalar_tensor_tensor(
                        out=ht[:, k, :], in0=xt[:, k, :], scalar=g_ffn(l, k), in1=rt[:],
                        op0=ALU.mult, op1=ALU.mult),
                        reads=[(xn, k), rn, "gains"], writes=[(hn, k)])
                for j in range(NJ):
                    wn, wt = load_w(wsl, wb_gu[l, j], [("wb_gu", l, j)], 2048)
                    gn, gt = psA.next()
                    upn, upt = psB.next()
                    for k in range(8):
                        P.op("pe", lambda e, k=k, wt=wt, gt=gt, ht=ht: e.matmul(
                            gt[:], wt[:, k * 256:k * 256 + 128], ht[:, k, :], start=(k == 0), stop=(k == 7)),
                            reads=[wn, (hn, k)], writes=[gn])
                    for k in range(8):
                        P.op("pe", lambda e, k=k, wt=wt, upt=upt, ht=ht: e.matmul(
                            upt[:], wt[:, k * 256 + 128:k * 256 + 256], ht[:, k, :], start=(k == 0), stop=(k == 7)),
                            reads=[wn, (hn, k)], writes=[upn])
                    sgn, sg = tf.next()
                    P.op("act", lambda e, sg=sg, gt=gt: e.activation(out=sg[:], in_=gt[:], func=AF.Silu),
                         reads=[gn], writes=[sgn])
                    P.op("dve", lambda e, j=j, sg=sg, upt=upt: e.tensor_tensor(
                        out=actT[:, j, :], in0=sg[:], in1=upt[:], op=ALU.mult),
                        reads=[sgn, upn], writes=[("act", j)])
                for j in range(8):
                    wn, wt = load_w(wdn, wb_dn[l, j], [("wb_dn", l, j)], NJ * 128)
                    pn, pt = psA.next()
                    for k in range(NJ):
                        P.op("pe", lambda e, k=k, wt=wt, pt=pt: e.matmul(
                            pt[:], wt[:, k * 128:(k + 1) * 128], actT[:, k, :], start=(k == 0), stop=(k == NJ - 1)),
                            reads=[wn, ("act", k)], writes=[pn])
                    P.op("dve", lambda e, j=j, xt=xt, pt=pt: e.tensor_tensor(
                        out=xt[:, j, :], in0=xt[:, j, :], in1=pt[:], op=ALU.add),
                        reads=[(xn, j), pn], writes=[(xn, j)])
                P.dma("pool", lambda e, xt=xt, t0=t0, dst=dst: e.dma_start(
                    out=dst[:, t0:t0 + 512].rearrange("(k p) t -> p k t", p=128), in_=xt[:]),
                    reads=[(xn, k) for k in range(8)], writes=[(dst_key, tc)], key="s_" + xn)

        P.emit(st)
    return nc


def _rope_tab(pos, dim):
    inv = (np.float32(10000.0) ** (-(np.arange(0, dim, 2, dtype=np.float32)) / np.float32(dim))).astype(np.float32)
    ang = pos.astype(np.float32)[:, None] * inv[None, :]
    ang = np.concatenate([ang, ang], axis=-1)
    return np.cos(ang).astype(np.float32), np.sin(ang).astype(np.float32)


def _rope_input(is_prompt_core):
    posP = np.arange(TP) if is_prompt_core else (np.arange(TP) % TS)
    pos = np.concatenate([posP, np.arange(TS)]).astype(np.int64)
    cr, sr = _rope_tab(pos // 64, 32)
    cc, sc = _rope_tab(pos % 64, 32)
    c1, s1 = _rope_tab(pos, 64)
    sgn32 = np.where(np.arange(32) < 16, -1.0, 1.0).astype(np.float32)
    sgn64 = np.where(np.arange(64) < 32, -1.0, 1.0).astype(np.float32)
    cosA = np.concatenate([cr, cc], axis=1)
    sinA = np.concatenate([sr * sgn32, sc * sgn32], axis=1)
    cosC = c1
    sinC = s1 * sgn64
    out = np.zeros((128, 6, NTOK), np.float32)
    for hh in range(2):
        out[hh * 64:(hh + 1) * 64, 0] = cosA.T
        out[hh * 64:(hh + 1) * 64, 1] = sinA.T
        out[hh * 64:(hh + 1) * 64, 2] = cosC.T
        out[hh * 64:(hh + 1) * 64, 3] = sinC.T
    out[0:64, 4] = 1.0
    out[0:64, 5] = 0.0
    out[64:128, 4] = cosC.T
    out[64:128, 5] = sinC.T
    return out


def _rperm_input():
    R = np.zeros((128, 3, 128), np.float32)
    for dst in range(128):
        d = dst % 64
        base = dst - d
        hb = (d // 32) * 32
        i = d % 32
        ip = i + 16 if i < 16 else i - 16
        R[base + hb + ip, 0, dst] = 1.0
        dp = d + 32 if d < 32 else d - 32
        R[base + dp, 1, dst] = 1.0
        if dst >= 64:
            R[base + dp, 2, dst] = 1.0
    return R.reshape(128, 384).astype(ml_dtypes.bfloat16)


def _bbank_input(rpb):
    kc = np.arange(64)[:, None]
    c = np.arange(64)[None, :]
    c0 = np.clip(c - 8, 0, 48)
    colvalid = (kc >= c0) & (kc < c0 + 16)
    dc = np.clip(kc - c + 15, 0, 30)
    NEG = np.float32(-1e30)
    out = np.full((DEPTH, 6, 128, 2, 15, 64), NEG, np.float32)
    for dr in range(15):
        Tb = np.where(colvalid[None, None], rpb[:, :, dr][:, :, dc], NEG)
        i0 = 14 - dr
        i1 = 15 - dr
        out[:, :, 0:64, 1, i0, :] = Tb
        if i1 <= 14:
            out[:, :, 64:128, 1, i1, :] = Tb
        if 3 <= dr <= 10:
            out[:, :, 0:64, 0, i0, :] = Tb
            out[:, :, 64:128, 0, i1, :] = Tb
    return out.reshape(DEPTH, 6, 128, 2 * 960)


def _bmask_tile(R, seq_rows, j, t):
    m = np.zeros((2, 64, 4, 64), np.float32)
    kt = 2 * j - 2 + t
    if 0 <= kt < R // 2:
        for qr in range(4):
            r = 4 * j + qr
            s = r // seq_rows
            rl = r - s * seq_rows
            r0 = min(max(rl - 4, 0), seq_rows - 8) + s * seq_rows
            for kr in range(2):
                key_r = 2 * kt + kr
                if r0 <= key_r < r0 + 8:
                    m[kr, :, qr, :] = 1.0
    return m.reshape(128, 256)


def _bmask_input(is_prompt_core):
    out = np.zeros((128, 36, 256), np.float32)
    idx = 0
    for j in B_SPECIAL[TP]:
        for t in range(6):
            out[:, idx] = _bmask_tile(64, 64 if is_prompt_core else 32, j, t)
            idx += 1
    for j in B_SPECIAL[TS]:
        for t in range(6):
            out[:, idx] = _bmask_tile(32, 32, j, t)
            idx += 1
    return out.reshape(128, 36 * 256).astype(ml_dtypes.bfloat16)


def _cmask_input():
    i = np.arange(128)[:, None]
    jq = np.arange(256)[None, :]
    out = np.zeros((128, 18, 2, 256), np.float32)
    for o in range(-8, 10):
        d = (jq - i) - 128 * o
        ad = np.abs(d)
        cnt = (ad <= 64).astype(np.float32) + ((d % 4 == 0) & (ad <= 256)) + ((d % 16 == 0) & (ad <= 1024))
        out[:, o + 8, 0] = cnt
        out[:, o + 8, 1] = cnt
    return out.reshape(128, 18 * 512).astype(ml_dtypes.bfloat16)


def _gains_input(norm_mix, norm_ffn, out_gain, q_gain, k_gain):
    g = np.zeros((128, 144), np.float32)
    for l in range(DEPTH):
        for k in range(8):
            g[:, l * 8 + k] = norm_mix[l, k * 128:(k + 1) * 128]
            g[:, 32 + l * 8 + k] = norm_ffn[l, k * 128:(k + 1) * 128]
            g[:, 64 + l * 8 + k] = out_gain[l, k * 128:(k + 1) * 128]
        for c in range(12):
            for hh in range(2):
                if c < 8:
                    head = 2 * c + hh
                    mix = 0 if head < 4 else (1 if head < 10 else 2)
                    g[hh * 64:(hh + 1) * 64, 96 + l * 12 + c] = q_gain[l, mix]
                else:
                    kv = 2 * (c - 8) + hh
                    mix = 0 if kv < 2 else (1 if kv < 5 else 2)
                    g[hh * 64:(hh + 1) * 64, 96 + l * 12 + c] = k_gain[l, mix]
    return g


_NC_CACHE = {}


def kernel(x_prompt, x_sample, norm_mix, w_in, q_gain, k_gain, rpb, out_gain, w_out, norm_ffn, w_gate_up, w_down):
    f = lambda a: np.ascontiguousarray(np.asarray(a, dtype=np.float32))
    x_prompt, x_sample = f(x_prompt), f(x_sample)
    w_in, w_out, w_gate_up, w_down = f(w_in), f(w_out), f(w_gate_up), f(w_down)
    rpb = f(rpb)
    w_in_t = np.ascontiguousarray(
        w_in[:, :, :1536].reshape(DEPTH, 8, 128, 12, 128).transpose(0, 3, 2, 1, 4)).reshape(DEPTH, 12, 128, 1024)
    w_v_t = np.ascontiguousarray(
        w_in[:, :, 1536:].reshape(DEPTH, 8, 128, 2, 256).transpose(0, 3, 2, 1, 4)).reshape(DEPTH, 2, 128, 2048)
    w_out_t = np.ascontiguousarray(
        w_out.reshape(DEPTH, 8, 128, 8, 128).transpose(0, 3, 2, 1, 4)).reshape(DEPTH, 8, 128, 1024)
    gu = w_gate_up.reshape(DEPTH, 8, 128, 2, NJ, 128)
    w_gu_t = np.ascontiguousarray(gu.transpose(0, 4, 2, 1, 3, 5)).reshape(DEPTH, NJ, 128, 2048)
    w_dn_t = np.ascontiguousarray(
        w_down.reshape(DEPTH, NJ, 128, 8, 128).transpose(0, 3, 2, 1, 4)).reshape(DEPTH, 8, 128, NJ * 128)
    gains = _gains_input(f(norm_mix), f(norm_ffn), f(out_gain), f(q_gain), f(k_gain))
    bbank = _bbank_input(rpb)
    rperm = _rperm_input()
    cmask = _cmask_input()
    shared = dict(w_in_t=w_in_t, w_v_t=w_v_t, w_out_t=w_out_t, w_gu_t=w_gu_t, w_dn_t=w_dn_t, gains=gains,
                  bbank=bbank, rperm=rperm, cmask=cmask)
    per_type = {}
    for ip in (True, False):
        per_type[ip] = dict(
            rope=_rope_input(ip), bmask=_bmask_input(ip),
            cross=np.full((128, 1), 0.0 if ip else -30000.0, np.float32))
    in_maps = []
    for c in range(8):
        if c < 4:
            toks = np.concatenate([x_prompt[c], x_sample[c]], axis=0)
        else:
            i = c - 4
            toks = np.concatenate([x_sample[8 + 2 * i], x_sample[9 + 2 * i], x_sample[4 + i]], axis=0)
        m = dict(shared)
        m.update(per_type[c < 4])
        m["xT"] = np.ascontiguousarray(toks.T)
        in_maps.append(m)
    if "nc" not in _NC_CACHE:
        _NC_CACHE["nc"] = build_nc()
    res = run_bass_kernel_spmd(_NC_CACHE["nc"], in_maps, core_ids=list(range(8)))
    y_prompt = np.zeros_like(x_prompt)
    y_sample = np.zeros_like(x_sample)
    for c in range(8):
        y = np.ascontiguousarray(res.results[c]["yT"].T)
        if c < 4:
            y_prompt[c] = y[:TP]
            y_sample[c] = y[TP:]
        else:
            i = c - 4
            y_sample[8 + 2 * i] = y[:TS]
            y_sample[9 + 2 * i] = y[TS:TP]
            y_sample[4 + i] = y[TP:]
    return (y_prompt, y_sample)
```

```python
import contextlib
import numpy as np
import ml_dtypes
import concourse.bass as bass
import concourse.mybir as mybir
from concourse.bass_utils import run_bass_kernel_spmd

F32 = mybir.dt.float32
BF = mybir.dt.bfloat16
AF = mybir.ActivationFunctionType
ALU = mybir.AluOpType

D = 1024
DEPTH = 4
NTOK = 6144
TP, TS = 4096, 2048
FFN = 2816
NJ = FFN // 128
EPS = 1e-6
SCALE = 0.125
SLOTS = ((0, TP), (TP, TS))
CHUNK_TYPE = {0: 0, 1: 0, 5: 1, 6: 1, 7: 1, 8: 0, 10: 2, 11: 1}
B_SPECIAL = {TP: (0, 7, 8, 15), TS: (0, 7)}
N_LAYERS_BUILD = DEPTH


class Prog:
    COMPUTE = ("pe", "act", "dve", "pool")
    EPOCH = 20000

    def __init__(self, nc):
        self.nc = nc
        self.ops = []
        self.last_w = {}
        self.readers = {}
        self.dma_count = {}

    def _add(self, eng, fn, reads, writes, dma_key=None):
        i = len(self.ops)
        deps = set()
        for b in reads:
            j = self.last_w.get(b)
            if j is not None:
                deps.add(j)
        for b in writes:
            j = self.last_w.get(b)
            if j is not None:
                deps.add(j)
            for j in self.readers.get(b, {}).values():
                deps.add(j)
        val = None
        if dma_key is not None:
            self.dma_count[dma_key] = self.dma_count.get(dma_key, 0) + 1
            val = 16 * self.dma_count[dma_key]
        self.ops.append([eng, fn, deps, dma_key, val, False])
        for b in writes:
            self.last_w[b] = i
            self.readers[b] = {}
        rk = eng if dma_key is None else ("dma", i)
        for b in reads:
            self.readers.setdefault(b, {})[rk] = i
        return i

    def op(self, eng, fn, reads=(), writes=()):
        return self._add(eng, fn, tuple(reads), tuple(writes))

    def dma(self, queue, fn, reads=(), writes=(), key=None):
        return self._add(queue, fn, tuple(reads), tuple(writes), dma_key=key)

    def emit(self, stack):
        nc = self.nc
        ops = self.ops
        for o in ops:
            for j in o[2]:
                pj = ops[j]
                if pj[3] is None:
                    if pj[0] == "pe" and o[0] == "pe" and o[3] is None:
                        continue
                    pj[5] = True
        sems = {}
        cnt = {e: 0 for e in self.COMPUTE}
        sig = {}
        for i, o in enumerate(ops):
            if o[3] is None:
                if o[5]:
                    cnt[o[0]] += 1
                    ep, v = divmod(cnt[o[0]] - 1, self.EPOCH)
                    sig[i] = ((o[0], ep), v + 1)
            else:
                sig[i] = (("dma", o[3]), o[4])
        for sk, _ in sig.values():
            if sk not in sems:
                sems[sk] = stack.enter_context(nc.semaphore("s%d" % len(sems)))
        per_eng = {}
        for i, o in enumerate(ops):
            per_eng.setdefault(o[0], []).append(i)
        engmap = {"pe": "tensor", "act": "scalar", "dve": "vector", "pool": "gpsimd", "sp": "sync"}
        final_waits = [(sems[("dma", k)], 16 * c) for k, c in self.dma_count.items()]
        block = stack.enter_context(nc.Block())

        def make(ename, idxs):
            def body(eng):
                waited = {}
                for i in idxs:
                    o = ops[i]
                    need = {}
                    for j in o[2]:
                        if j not in sig:
                            continue
                        sk, v = sig[j]
                        if v > need.get(sk, 0):
                            need[sk] = v
                    for sk, v in need.items():
                        if waited.get(sk, 0) >= v:
                            continue
                        eng.wait_ge(sems[sk], v)
                        waited[sk] = v
                    ins = o[1](eng)
                    if i in sig:
                        sk, v = sig[i]
                        ins.then_inc(sems[sk], 16 if o[3] is not None else 1)
                if ename == "sp":
                    for s, v in final_waits:
                        eng.wait_ge(s, v)
            return body

        if "sp" not in per_eng:
            per_eng["sp"] = []
        for ename, idxs in per_eng.items():
            getattr(block, engmap[ename])(make(ename, idxs))


class Rot:
    def __init__(self, items):
        self.items = list(items)
        self.i = -1

    def next(self):
        self.i = (self.i + 1) % len(self.items)
        return self.items[self.i]


def build_nc():
    nc = bass.Bass("TRN2", target_bir_lowering=False)

    def din(name, shape, dt=F32):
        return nc.dram_tensor(name, list(shape), dt, kind="ExternalInput").ap()

    def dscr(name, shape, dt):
        return nc.dram_tensor(name, list(shape), dt, kind="Internal").ap()

    xT = din("xT", [D, NTOK])
    yT = nc.dram_tensor("yT", [D, NTOK], F32, kind="ExternalOutput").ap()
    w_in_t = din("w_in_t", [DEPTH, 12, 128, 8 * 128])
    w_v_t = din("w_v_t", [DEPTH, 2, 128, 8 * 256])
    w_out_t = din("w_out_t", [DEPTH, 8, 128, 8 * 128])
    w_gu_t = din("w_gu_t", [DEPTH, NJ, 128, 8 * 256])
    w_dn_t = din("w_dn_t", [DEPTH, 8, 128, NJ * 128])
    gains_d = din("gains", [128, 3 * 32 + 48])
    rope_d = din("rope", [128, 6, NTOK])
    rperm_d = din("rperm", [128, 3 * 128], BF)
    bbank_d = din("bbank", [DEPTH, 6, 128, 2 * 960])
    bmask_d = din("bmask", [128, 36 * 256], BF)
    cmask_d = din("cmask", [128, 18 * 512], BF)
    cross_d = din("cross", [128, 1])

    wb_in = dscr("wb_in", [DEPTH, 12, 128, 8 * 128], BF)
    wb_v = dscr("wb_v", [DEPTH, 2, 128, 8 * 256], BF)
    wb_out = dscr("wb_out", [DEPTH, 8, 128, 8 * 128], BF)
    wb_gu = dscr("wb_gu", [DEPTH, NJ, 128, 8 * 256], BF)
    wb_dn = dscr("wb_dn", [DEPTH, 8, 128, NJ * 128], BF)
    xres = dscr("xres", [D, NTOK], F32)
    qkT = dscr("qkT", [12 * 128, NTOK], BF)
    Vd = dscr("Vd", [NTOK, 512], BF)
    oT = dscr("oT", [D, NTOK], F32)

    st = contextlib.ExitStack()
    with st:
        def sb(name, shape, dt):
            return st.enter_context(nc.sbuf_tensor("sb_" + name, list(shape), dt))

        def psum(name):
            return st.enter_context(nc.psum_tensor(name, [128, 512], F32))

        P = Prog(nc)
        uid = [0]

        def U(prefix):
            uid[0] += 1
            return (prefix, uid[0])

        gains = sb("gains", [128, 144], F32)
        rperm = sb("rperm", [128, 3, 128], BF)
        ones_bf = sb("ones_bf", [128, 128], BF)
        bones = sb("bones", [128, 128], BF)
        sel65 = sb("sel65", [65, 64], F32)
        cmask = sb("cmask", [128, 18, 512], BF)
        bmk = sb("bmk", [128, 6, 256], BF)

        P.dma("sp", lambda e: e.dma_start(out=gains[:], in_=gains_d), writes=["gains"], key="c_gains")
        P.dma("sp", lambda e: e.dma_start(out=rperm[:], in_=rperm_d.rearrange("p (a b) -> p a b", a=3)),
              writes=["rperm"], key="c_rperm")
        P.dma("sp", lambda e: e.dma_start(out=cmask[:], in_=cmask_d.rearrange("p (a b) -> p a b", a=18)),
              writes=["cmask"], key="c_cmask")
        P.op("pool", lambda e: e.memset(ones_bf[:], 1.0), writes=["ones_bf"])
        P.op("pool", lambda e: e.memset(bones[:], 0.0), writes=["bones"])
        P.op("pool", lambda e: e.memset(bones[0:64, 0:64], 1.0), reads=[], writes=["bones"])
        P.op("pool", lambda e: e.memset(bones[64:128, 64:128], 1.0), reads=[], writes=["bones"])
        P.op("pool", lambda e: e.memset(sel65[:], 0.0), writes=["sel65"])
        P.op("pool", lambda e: e.memset(sel65[64:65, :], 1.0), writes=["sel65"])

        def g_mix(l, k):
            return gains[:, l * 8 + k: l * 8 + k + 1]

        def g_ffn(l, k):
            return gains[:, 32 + l * 8 + k: 32 + l * 8 + k + 1]

        def g_out(l, k):
            return gains[:, 64 + l * 8 + k: 64 + l * 8 + k + 1]

        def g_qk(l, c):
            return gains[:, 96 + l * 12 + c: 96 + l * 12 + c + 1]

        def cast_w(src, dst, l, name):
            s2 = src[l]
            d2 = dst[l]
            nd = len(s2.shape)
            if nd == 3:
                s2 = s2.rearrange("a p n -> (a p) n")
                d2 = d2.rearrange("a p n -> (a p) n")
            rows = s2.shape[0]
            step = 512
            for r0 in range(0, rows, step):
                r1 = min(rows, r0 + step)
                P.dma("pool", lambda e, a=d2[r0:r1, :], b=s2[r0:r1, :]: e.dma_start(out=a, in_=b),
                      writes=[(name, l, r0 // 128 + i) for i in range((r1 - r0 + 127) // 128)],
                      key=("cast", name, l, r0))

        for l in range(N_LAYERS_BUILD):
            cast_w(w_in_t, wb_in, l, "wb_in")
            cast_w(w_v_t, wb_v, l, "wb_v")
            cast_w(w_out_t, wb_out, l, "wb_out")
            cast_w(w_gu_t, wb_gu, l, "wb_gu")
            cast_w(w_dn_t, wb_dn, l, "wb_dn")

        bank = [("ps%d" % i, psum("ps%d" % i)) for i in range(8)]
        psA = Rot(bank[0:3])
        psB = Rot(bank[3:5])
        psZ = bank[5]
        psN = bank[6]
        psH = bank[7]
        psS = Rot([bank[0], bank[1], bank[2], bank[6], bank[7]])

        xa = Rot([("xa%d" % i, sb("xa%d" % i, [128, 8, 512], F32)) for i in range(2)])
        hTr = Rot([("hT%d" % i, sb("hT%d" % i, [128, 8, 512], BF)) for i in range(2)])
        ropeR = Rot([("rope%d" % i, sb("rope%d" % i, [128, 6, 512], F32)) for i in range(2)])
        wsl = Rot([("w%d" % i, sb("w%d" % i, [128, 8 * 256], BF)) for i in range(3)])
        wdn = Rot([("wd%d" % i, sb("wd%d" % i, [128, NJ * 128], BF)) for i in range(2)])
        actT = sb("actT", [128, NJ, 512], BF)
        sqb = Rot([("sqb%d" % i, sb("sqb%d" % i, [128, 512], BF)) for i in range(2)])
        tf = Rot([("tf%d" % i, sb("tf%d" % i, [128, 512], F32)) for i in range(4)])
        tb = Rot([("tb%d" % i, sb("tb%d" % i, [128, 512], BF)) for i in range(5)])
        rstd3 = [("rs%d" % i, sb("rs%d" % i, [128, 512], F32)) for i in range(3)]
        sqc = Rot([("sqc%d" % i, sb("sqc%d" % i, [128, 512], BF)) for i in range(2)])
        stq = Rot([("stq%d" % i, sb("stq%d" % i, [128, 512], BF)) for i in range(2)])
        stv = Rot([("stv%d" % i, sb("stv%d" % i, [128, 4, 512], BF)) for i in range(1)])
        KT = Rot([("KT%d" % i, sb("KT%d" % i, [128, TP], BF)) for i in range(1)])
        QT = Rot([("QT%d" % i, sb("QT%d" % i, [128, 256], BF)) for i in range(3)])
        Vt = Rot([("Vt%d" % i, sb("Vt%d" % i, [128, 32, 128], BF)) for i in range(1)])
        ebR = Rot([("eb%d" % i, sb("eb%d" % i, [128, 2, 960], F32)) for i in range(2)])
        uf = Rot([("uf%d" % i, sb("uf%d" % i, [65, 512], F32)) for i in range(2)])
        ost = Rot([("ost%d" % i, sb("ost%d" % i, [64, 512], F32)) for i in range(1)])

        crossb = sb("crossb", [128, 1], F32)
        P.dma("sp", lambda e: e.dma_start(out=crossb[:], in_=cross_d), writes=["crossb"], key="c_crossb")
        for (vn, vt_) in Vt.items:
            P.op("pool", lambda e, t=vt_: e.memset(t[:, :, 64:128], 0.0), writes=[vn])
            P.op("pool", lambda e, t=vt_: e.memset(t[:, :, 64:65], 1.0), writes=[vn])

        class Pipe:
            def __init__(self):
                self.step = 0
                self.q = []
                self.seq = 0

            def defer(self, lag, fn):
                self.q.append((self.step + lag, self.seq, fn))
                self.seq += 1

            def tick(self):
                self.step += 1
                self.flush()

            def flush(self, everything=False):
                while True:
                    ready = [x for x in self.q if everything or x[0] <= self.step]
                    if not ready:
                        break
                    ready.sort()
                    x = ready[0]
                    self.q.remove(x)
                    x[2]()

        pipe = Pipe()

        def load_w(rot, src2d, deps, ncols):
            name, t = rot.next()
            P.dma("sp", lambda e, t=t, s=src2d, n=ncols: e.dma_start(out=t[:, 0:n], in_=s),
                  reads=deps, writes=[name], key="l_" + name)
            return name, t

        def recip(out_ap, in_ap, reads, writes, npart=128, n=512):
            P.op("dve", lambda e: e.reciprocal(out=out_ap, in_=in_ap), reads=list(reads), writes=list(writes))

        def rms_rstd(src_name, src_t, nk, groups, inv_dims):
            outs = []
            for gi, grp in enumerate(groups):
                pn, pt = psN
                for ii, k in enumerate(grp):
                    sn, s_ = sqb.next()
                    P.op("act", lambda e, s_=s_, k=k: e.activation(out=s_[:], in_=src_t[:, k, :], func=AF.Square),
                         reads=[(src_name, k)], writes=[sn])
                    P.op("pe", lambda e, s_=s_, a=(ii == 0), b=(ii == len(grp) - 1), pt=pt:
                         e.matmul(pt[:], ones_bf[:], s_[:], start=a, stop=b),
                         reads=[sn, "ones_bf"], writes=[pn])
                rn, rt = rstd3[gi]
                tn, tt_ = tf.next()
                P.op("act", lambda e, tt_=tt_, pt=pt, sc=inv_dims[gi]:
                     e.activation(out=tt_[:], in_=pt[:], func=AF.Ln, scale=sc, bias=EPS),
                     reads=[pn], writes=[tn])
                P.op("act", lambda e, tt_=tt_, rt=rt: e.activation(out=rt[:], in_=tt_[:], func=AF.Exp, scale=-0.5),
                     reads=[tn], writes=[rn])
                outs.append((rn, rt))
            return outs

        for l in range(N_LAYERS_BUILD):
            src = xT if l == 0 else xres
            dst = yT if l == N_LAYERS_BUILD - 1 else xres
            src_key = "xin" if l == 0 else "xres"
            dst_key = "yT" if l == N_LAYERS_BUILD - 1 else "xres"

            def prep(tc, l=l, src=src, src_key=src_key):
                t0 = tc * 512
                xn, xt = xa.next()
                P.dma("sp", lambda e, xt=xt, t0=t0, src=src: e.dma_start(
                    out=xt[:], in_=src[:, t0:t0 + 512].rearrange("(k p) t -> p k t", p=128)),
                    reads=[(src_key, tc)], writes=[(xn, k) for k in range(8)], key="l_" + xn)
                rpn, rpt = ropeR.next()
                P.dma("sp", lambda e, t0=t0, rpt=rpt: e.dma_start(out=rpt[:], in_=rope_d[:, :, t0:t0 + 512]),
                      writes=[rpn], key="l_" + rpn)
                (rn, rt), = rms_rstd(xn, xt, 8, [list(range(8))], [1.0 / D])
                hn, ht = hTr.next()
                for k in range(8):
                    P.op("dve", lambda e, k=k, xt=xt, rt=rt, l=l, ht=ht: e.scalar_tensor_tensor(
                        out=ht[:, k, :], in0=xt[:, k, :], scalar=g_mix(l, k), in1=rt[:],
                        op0=ALU.mult, op1=ALU.mult),
                        reads=[(xn, k), rn, "gains"], writes=[(hn, k)])
                return hn, ht, rpn, rpt

            def qk_chunk(tc, c, hn, ht, rpn, rpt, l=l):
                t0 = tc * 512
                wn, wt = load_w(wsl, wb_in[l, c], [("wb_in", l, c)], 1024)
                pn, pt = psA.next()
                for k in range(8):
                    P.op("pe", lambda e, k=k, wt=wt, pt=pt: e.matmul(
                        pt[:], wt[:, k * 128:(k + 1) * 128], ht[:, k, :], start=(k == 0), stop=(k == 7)),
                        reads=[wn, (hn, k)], writes=[pn])
                sn, s_ = sqc.next()
                P.op("act", lambda e, s_=s_, pt=pt: e.activation(out=s_[:], in_=pt[:], func=AF.Square),
                     reads=[pn], writes=[sn])
                ty = CHUNK_TYPE.get(c)
                state = {}

                def part2():
                    hhn, hht = psH
                    P.op("pe", lambda e, s_=s_, hht=hht: e.matmul(hht[:], bones[:], s_[:], start=True, stop=True),
                         reads=[sn, "bones"], writes=[hhn])
                    t1n, t1 = tf.next()
                    P.op("act", lambda e, t1=t1, hht=hht: e.activation(out=t1[:], in_=hht[:], func=AF.Ln,
                                                                       scale=1.0 / 64, bias=EPS),
                         reads=[hhn], writes=[t1n])
                    r2n, r2 = tf.next()
                    P.op("act", lambda e, t1=t1, r2=r2: e.activation(out=r2[:], in_=t1[:], func=AF.Exp, scale=-0.5),
                         reads=[t1n], writes=[r2n])
                    qsn, qs = stq.next()
                    state["qs"] = (qsn, qs)
                    if ty is None:
                        P.op("dve", lambda e, qs=qs, r2=r2: e.scalar_tensor_tensor(
                            out=qs[:], in0=pt[:], scalar=g_qk(l, c), in1=r2[:], op0=ALU.mult, op1=ALU.mult),
                            reads=[pn, r2n, "gains"], writes=[qsn])
                        store(qsn, qs)
                    else:
                        qfn, qf = tf.next()
                        P.op("dve", lambda e, qf=qf, r2=r2: e.scalar_tensor_tensor(
                            out=qf[:], in0=pt[:], scalar=g_qk(l, c), in1=r2[:], op0=ALU.mult, op1=ALU.mult),
                            reads=[pn, r2n, "gains"], writes=[qfn])
                        qbn, qb_ = tb.next()
                        P.op("act", lambda e, qb_=qb_, qf=qf: e.activation(out=qb_[:], in_=qf[:], func=AF.Copy),
                             reads=[qfn], writes=[qbn])
                        an, a_ = tf.next()
                        P.op("dve", lambda e, a_=a_, qf=qf: e.tensor_tensor(
                            out=a_[:], in0=qf[:], in1=rpt[:, 2 * ty, :], op=ALU.mult),
                            reads=[qfn, rpn], writes=[an])
                        state["rot"] = (qbn, qb_, an, a_)

                def store(qsn, qs):
                    P.dma("pool", lambda e, qs=qs: e.dma_start(
                        out=qkT[c * 128:(c + 1) * 128, t0:t0 + 512], in_=qs[:]),
                        reads=[qsn], writes=[("qkT", c, tc)], key="s_" + qsn)

                def part3():
                    qbn, qb_, an, a_ = state["rot"]
                    qsn, qs = state["qs"]
                    rbn, rb = psB.next()
                    P.op("pe", lambda e, rb=rb, qb_=qb_: e.matmul(rb[:], rperm[:, ty, :], qb_[:], start=True, stop=True),
                         reads=[qbn, "rperm"], writes=[rbn])
                    bn, b_ = tf.next()
                    P.op("dve", lambda e, b_=b_, rb=rb: e.tensor_tensor(
                        out=b_[:], in0=rb[:], in1=rpt[:, 2 * ty + 1, :], op=ALU.mult),
                        reads=[rbn, rpn], writes=[bn])
                    P.op("dve", lambda e, qs=qs, a_=a_, b_=b_: e.tensor_tensor(
                        out=qs[:], in0=a_[:], in1=b_[:], op=ALU.add),
                        reads=[an, bn], writes=[qsn])
                    store(qsn, qs)

                pipe.defer(2, part2)
                if ty is not None:
                    pipe.defer(3, part3)
                pipe.tick()

            def v_proj(tc, hn, ht, l=l):
                t0 = tc * 512
                svn, sv = stv.next()
                for half in range(2):
                    wn, wt = load_w(wsl, wb_v[l, half], [("wb_v", l, half)], 2048)
                    for tt in range(4):
                        pn, pt = psA.next()
                        for k in range(8):
                            P.op("pe", lambda e, k=k, tt=tt, pt=pt, wt=wt: e.matmul(
                                pt[:, 0:256], ht[:, k, tt * 128:(tt + 1) * 128], wt[:, k * 256:(k + 1) * 256],
                                start=(k == 0), stop=(k == 7)),
                                reads=[wn, (hn, k)], writes=[pn])
                        P.op("act", lambda e, sv=sv, tt=tt, pt=pt, half=half: e.activation(
                            out=sv[:, tt, half * 256:(half + 1) * 256], in_=pt[:, 0:256], func=AF.Copy),
                            reads=[pn], writes=[(svn, tt, half)])
                        pipe.tick()
                P.dma("pool", lambda e, sv=sv: e.dma_start(
                    out=Vd[t0:t0 + 512, :].rearrange("(tt p) n -> p tt n", p=128), in_=sv[:]),
                    reads=[(svn, tt, hf) for tt in range(4) for hf in range(2)], writes=[("Vd", tc)], key="s_" + svn)

            ntc = NTOK // 512
            cur = prep(0)
            for tc in range(ntc):
                hn, ht, rpn, rpt = cur
                for c in range(12):
                    qk_chunk(tc, c, hn, ht, rpn, rpt)
                    if c == 5 and tc + 1 < ntc:
                        cur = prep(tc + 1)
                v_proj(tc, hn, ht)
            pipe.flush(everything=True)

            def attend(base, T, g, heads, mixer, l=l):
                ntile = T // 128
                tcs = range(base // 512, (base + T) // 512)
                pipe.flush(everything=True)
                kn, kt_ = KT.next()
                for hh in range(2):
                    P.dma("sp", lambda e, kt_=kt_, hh=hh: e.dma_start(
                        out=kt_[hh * 64:(hh + 1) * 64, 0:T],
                        in_=qkT[1024 + g * 64:1024 + (g + 1) * 64, base:base + T]),
                        reads=[("qkT", 8 + g // 2, tc) for tc in tcs], writes=[kn], key="l_" + kn)
                vn, vt_ = Vt.next()
                P.dma("sp", lambda e, vt_=vt_: e.dma_start(
                    out=vt_[:, 0:ntile, 0:64],
                    in_=Vd[base:base + T, g * 64:(g + 1) * 64].rearrange("(n p) d -> p n d", p=128)),
                    reads=[("Vd", tc) for tc in tcs], writes=[vn], key="l_" + vn)
                ebs = []
                if mixer == 1:
                    for h in heads:
                        ebn, ebt = ebR.next()
                        P.dma("sp", lambda e, h=h, ebt=ebt: e.dma_start(
                            out=ebt[:], in_=bbank_d[l, h - 4].rearrange("p (a b) -> p a b", a=2)),
                            writes=[ebn], key="l_" + ebn)
                        P.op("act", lambda e, ebt=ebt: e.activation(out=ebt[:], in_=ebt[:], func=AF.Exp),
                             reads=[ebn], writes=[ebn])
                        ebs.append((ebn, ebt))
                N = 256
                for qb in range(T // N):
                    q0 = qb * N
                    qn, qt_ = QT.next()
                    P.dma("sp", lambda e, qt_=qt_, q0=q0: e.dma_start(
                        out=qt_[:, 0:N], in_=qkT[g * 128:(g + 1) * 128, base + q0:base + q0 + N]),
                        reads=[("qkT", g, (base + q0) // 512)], writes=[qn], key="l_" + qn)
                    tiles = []
                    if mixer == 0:
                        for kt in range(ntile):
                            cross = (T == TP) and ((q0 < TS) != (kt < 16))
                            tiles.append((kt, cross, None, None))
                    elif mixer == 2:
                        for kt in range(ntile):
                            o = kt - 2 * qb
                            if -8 <= o <= 9:
                                cross = (T == TP) and ((q0 < TS) != (kt < 16))
                                tiles.append((kt, cross, ("c", o + 8), None))
                    else:
                        spec = B_SPECIAL[T]
                        is_spec = qb in spec
                        if is_spec:
                            mi = (0 if T == TP else 4) + spec.index(qb)
                            P.dma("sp", lambda e, mi=mi: e.dma_start(
                                out=bmk[:], in_=bmask_d[:, mi * 6 * 256:(mi + 1) * 6 * 256].rearrange(
                                    "p (a b) -> p a b", a=6)),
                                writes=["bmk"], key="l_bmk")
                        for t in range(6):
                            kt = 2 * qb - 2 + t
                            if 0 <= kt < ntile:
                                tiles.append((kt, False, ("b", 1 if is_spec else 0, t), t if is_spec else None))
                    un, ut = psB.next()
                    for ti, (kt, cross, m1, m2) in enumerate(tiles):
                        sc = [psS.next(), psS.next()]
                        for hh in range(2):
                            P.op("pe", lambda e, s_=sc[hh][1], kt=kt, qt_=qt_, hh=hh: e.matmul(
                                s_[:, 0:256],
                                kt_[hh * 64:(hh + 1) * 64, kt * 128:(kt + 1) * 128],
                                qt_[hh * 64:(hh + 1) * 64, 0:256],
                                start=True, stop=True),
                                reads=[kn, qn], writes=[sc[hh][0]])
                        pbn, pb = tb.next()
                        if m1 is None:
                            dstn, dstt = pbn, pb
                        else:
                            dstn, dstt = tf.next()
                        for hh in range(2):
                            if cross:
                                P.op("act", lambda e, dstt=dstt, s_=sc[hh][1], hh=hh: e.activation(
                                    out=dstt[:, hh * 256:(hh + 1) * 256], in_=s_[:, 0:256], func=AF.Exp,
                                    scale=SCALE, bias=crossb[:, 0:1]),
                                    reads=[sc[hh][0], "crossb"], writes=[dstn])
                            else:
                                P.op("act", lambda e, dstt=dstt, s_=sc[hh][1], hh=hh: e.activation(
                                    out=dstt[:, hh * 256:(hh + 1) * 256], in_=s_[:, 0:256], func=AF.Exp,
                                    scale=SCALE),
                                    reads=[sc[hh][0]], writes=[dstn])
                        if m1 is not None:
                            ef = dstt
                            efn = dstn
                            if m1[0] == "c":
                                P.op("dve", lambda e, pb=pb, ef=ef, oi=m1[1]: e.tensor_tensor(
                                    out=pb[:], in0=ef[:], in1=cmask[:, oi, :], op=ALU.mult),
                                    reads=[efn, "cmask"], writes=[pbn])
                            else:
                                v, t = m1[1], m1[2]
                                i0 = (11 - 2 * t) * 64
                                if m2 is None:
                                    for hh in range(2):
                                        ebn, ebt = ebs[hh]
                                        P.op("pool" if hh == 0 else "dve", lambda e, pb=pb, ef=ef, v=v, i0=i0, hh=hh, ebt=ebt: e.tensor_tensor(
                                            out=pb[:, hh * 256:(hh + 1) * 256], in0=ef[:, hh * 256:(hh + 1) * 256],
                                            in1=ebt[:, v, i0:i0 + 256], op=ALU.mult),
                                            reads=[efn, ebn], writes=[(pbn, "h", hh)])
                                else:
                                    e2n, e2 = tf.next()
                                    for hh in range(2):
                                        ebn, ebt = ebs[hh]
                                        P.op("dve", lambda e, e2=e2, ef=ef, v=v, i0=i0, hh=hh, ebt=ebt: e.tensor_tensor(
                                            out=e2[:, hh * 256:(hh + 1) * 256], in0=ef[:, hh * 256:(hh + 1) * 256],
                                            in1=ebt[:, v, i0:i0 + 256], op=ALU.mult),
                                            reads=[efn, ebn], writes=[e2n])
                                    for hh in range(2):
                                        P.op("pool", lambda e, pb=pb, e2=e2, t=m2, hh=hh: e.tensor_tensor(
                                            out=pb[:, hh * 256:(hh + 1) * 256], in0=e2[:, hh * 256:(hh + 1) * 256],
                                            in1=bmk[:, t, :], op=ALU.mult),
                                            reads=[e2n, "bmk"], writes=[pbn])

                        def pv(ut=ut, un=un, kt=kt, pb=pb, pbn=pbn, a=(ti == 0), b=(ti == len(tiles) - 1)):
                            P.op("pe", lambda e: e.matmul(
                                ut[:, :], vt_[:, kt, :], pb[:], start=a, stop=b),
                                reads=[vn, pbn, (pbn, "h", 0), (pbn, "h", 1)], writes=[un])
                        pipe.defer(4, pv)
                        pipe.tick()

                    def fin1(ut=ut, un=un, q0=q0):
                        ufn, uft = uf.next()
                        P.op("act", lambda e: e.activation(out=uft[:], in_=ut[0:65, :], func=AF.Copy),
                             reads=[un], writes=[ufn])

                        def fin2():
                            zn, zt = psZ
                            P.op("pe", lambda e: e.matmul(zt[0:64, :], sel65[:], uft[:], start=True, stop=True),
                                 reads=[ufn, "sel65"], writes=[zn])
                            rzn, rz = tf.next()
                            recip(rz[0:64, :], zt[0:64, :], [zn], [rzn], npart=64, n=512)
                            on, ot = ost.next()
                            P.op("pool", lambda e: e.tensor_tensor(
                                out=ot[:], in0=uft[0:64, :], in1=rz[0:64, :], op=ALU.mult),
                                reads=[ufn, rzn], writes=[on])
                            tcq = (base + q0) // 512
                            sub = (q0 // 256) % 2
                            P.dma("pool", lambda e: e.dma_start(
                                out=oT[2 * g * 64:(2 * g + 2) * 64, base + q0:base + q0 + 256].rearrange(
                                    "(h d) q -> d h q", h=2),
                                in_=ot[:].rearrange("d (h q) -> d h q", h=2)),
                                reads=[on], writes=[("oT", 2 * g, tcq, sub), ("oT", 2 * g + 1, tcq, sub)],
                                key="s_" + on)
                        pipe.defer(3, fin2)
                    pipe.defer(3, fin1)

            for (base, T) in SLOTS:
                for g in range(8):
                    mixer = 0 if g < 2 else (1 if g < 5 else 2)
                    attend(base, T, g, (2 * g, 2 * g + 1), mixer)
            pipe.flush(everything=True)

            for tc in range(NTOK // 512):
                t0 = tc * 512
                on_, ot_ = xa.next()
                oreads = []
                for h in range(16):
                    oreads += [("oT", h, tc, 0), ("oT", h, tc, 1)]
                P.dma("sp", lambda e, ot_=ot_, t0=t0: e.dma_start(
                    out=ot_[:], in_=oT[:, t0:t0 + 512].rearrange("(k p) t -> p k t", p=128)),
                    reads=oreads, writes=[(on_, k) for k in range(8)], key="l_" + on_)
                rs = rms_rstd(on_, ot_, 8, [[0, 1], [2, 3, 4], [5, 6, 7]], [1.0 / 256, 1.0 / 384, 1.0 / 384])
                hn, ht = hTr.next()
                grp_of = [0, 0, 1, 1, 1, 2, 2, 2]
                for k in range(8):
                    rn, rt = rs[grp_of[k]]
                    P.op("dve", lambda e, k=k, ot_=ot_, rt=rt, l=l, ht=ht: e.scalar_tensor_tensor(
                        out=ht[:, k, :], in0=ot_[:, k, :], scalar=g_out(l, k), in1=rt[:],
                        op0=ALU.mult, op1=ALU.mult),
                        reads=[(on_, k), rn, "gains"], writes=[(hn, k)])
                xn, xt = xa.next()
                P.dma("sp", lambda e, xt=xt, t0=t0, src=src: e.dma_start(
                    out=xt[:], in_=src[:, t0:t0 + 512].rearrange("(k p) t -> p k t", p=128)),
                    reads=[(src_key, tc)], writes=[(xn, k) for k in range(8)], key="l_" + xn)
                for j in range(8):
                    wn, wt = load_w(wsl, wb_out[l, j], [("wb_out", l, j)], 1024)
                    pn, pt = psA.next()
                    for k in range(8):
                        P.op("pe", lambda e, k=k, wt=wt, pt=pt, ht=ht: e.matmul(
                            pt[:], wt[:, k * 128:(k + 1) * 128], ht[:, k, :], start=(k == 0), stop=(k == 7)),
                            reads=[wn, (hn, k)], writes=[pn])
                    P.op("dve", lambda e, j=j, xt=xt, pt=pt: e.tensor_tensor(
                        out=xt[:, j, :], in0=xt[:, j, :], in1=pt[:], op=ALU.add),
                        reads=[(xn, j), pn], writes=[(xn, j)])
                (rn, rt), = rms_rstd(xn, xt, 8, [list(range(8))], [1.0 / D])
                hn, ht = hTr.next()
                for k in range(8):
                    P.op("dve", lambda e, k=k, xt=xt, rt=rt, l=l, ht=ht: e.scalar_tensor_tensor(
                        out=ht[:, k, :], in0=xt[:, k, :], scalar=g_ffn(l, k), in1=rt[:],
                        op0=ALU.mult, op1=ALU.mult),
                        reads=[(xn, k), rn, "gains"], writes=[(hn, k)])
                for j in range(NJ):
                    wn, wt = load_w(wsl, wb_gu[l, j], [("wb_gu", l, j)], 2048)
                    gn, gt = psA.next()
                    upn, upt = psB.next()
                    for k in range(8):
                        P.op("pe", lambda e, k=k, wt=wt, gt=gt, ht=ht: e.matmul(
                            gt[:], wt[:, k * 256:k * 256 + 128], ht[:, k, :], start=(k == 0), stop=(k == 7)),
                            reads=[wn, (hn, k)], writes=[gn])
                    for k in range(8):
                        P.op("pe", lambda e, k=k, wt=wt, upt=upt, ht=ht: e.matmul(
                            upt[:], wt[:, k * 256 + 128:k * 256 + 256], ht[:, k, :], start=(k == 0), stop=(k == 7)),
                            reads=[wn, (hn, k)], writes=[upn])
                    sgn, sg = tf.next()
                    P.op("act", lambda e, sg=sg, gt=gt: e.activation(out=sg[:], in_=gt[:], func=AF.Silu),
                         reads=[gn], writes=[sgn])
                    P.op("dve", lambda e, j=j, sg=sg, upt=upt: e.tensor_tensor(
                        out=actT[:, j, :], in0=sg[:], in1=upt[:], op=ALU.mult),
                        reads=[sgn, upn], writes=[("act", j)])
                for j in range(8):
                    wn, wt = load_w(wdn, wb_dn[l, j], [("wb_dn", l, j)], NJ * 128)
                    pn, pt = psA.next()
                    for k in range(NJ):
                        P.op("pe", lambda e, k=k, wt=wt, pt=pt: e.matmul(
                            pt[:], wt[:, k * 128:(k + 1) * 128], actT[:, k, :], start=(k == 0), stop=(k == NJ - 1)),
                            reads=[wn, ("act", k)], writes=[pn])
                    P.op("dve", lambda e, j=j, xt=xt, pt=pt: e.tensor_tensor(
                        out=xt[:, j, :], in0=xt[:, j, :], in1=pt[:], op=ALU.add),
                        reads=[(xn, j), pn], writes=[(xn, j)])
                P.dma("pool", lambda e, xt=xt, t0=t0, dst=dst: e.dma_start(
                    out=dst[:, t0:t0 + 512].rearrange("(k p) t -> p k t", p=128), in_=xt[:]),
                    reads=[(xn, k) for k in range(8)], writes=[(dst_key, tc)], key="s_" + xn)

        P.emit(st)
    return nc


def _rope_tab(pos, dim):
    inv = (np.float32(10000.0) ** (-(np.arange(0, dim, 2, dtype=np.float32)) / np.float32(dim))).astype(np.float32)
    ang = pos.astype(np.float32)[:, None] * inv[None, :]
    ang = np.concatenate([ang, ang], axis=-1)
    return np.cos(ang).astype(np.float32), np.sin(ang).astype(np.float32)


def _rope_input(is_prompt_core):
    posP = np.arange(TP) if is_prompt_core else (np.arange(TP) % TS)
    pos = np.concatenate([posP, np.arange(TS)]).astype(np.int64)
    cr, sr = _rope_tab(pos // 64, 32)
    cc, sc = _rope_tab(pos % 64, 32)
    c1, s1 = _rope_tab(pos, 64)
    sgn32 = np.where(np.arange(32) < 16, -1.0, 1.0).astype(np.float32)
    sgn64 = np.where(np.arange(64) < 32, -1.0, 1.0).astype(np.float32)
    cosA = np.concatenate([cr, cc], axis=1)
    sinA = np.concatenate([sr * sgn32, sc * sgn32], axis=1)
    cosC = c1
    sinC = s1 * sgn64
    out = np.zeros((128, 6, NTOK), np.float32)
    for hh in range(2):
        out[hh * 64:(hh + 1) * 64, 0] = cosA.T
        out[hh * 64:(hh + 1) * 64, 1] = sinA.T
        out[hh * 64:(hh + 1) * 64, 2] = cosC.T
        out[hh * 64:(hh + 1) * 64, 3] = sinC.T
    out[0:64, 4] = 1.0
    out[0:64, 5] = 0.0
    out[64:128, 4] = cosC.T
    out[64:128, 5] = sinC.T
    return out


def _rperm_input():
    R = np.zeros((128, 3, 128), np.float32)
    for dst in range(128):
        d = dst % 64
        base = dst - d
        hb = (d // 32) * 32
        i = d % 32
        ip = i + 16 if i < 16 else i - 16
        R[base + hb + ip, 0, dst] = 1.0
        dp = d + 32 if d < 32 else d - 32
        R[base + dp, 1, dst] = 1.0
        if dst >= 64:
            R[base + dp, 2, dst] = 1.0
    return R.reshape(128, 384).astype(ml_dtypes.bfloat16)


def _bbank_input(rpb):
    kc = np.arange(64)[:, None]
    c = np.arange(64)[None, :]
    c0 = np.clip(c - 8, 0, 48)
    colvalid = (kc >= c0) & (kc < c0 + 16)
    dc = np.clip(kc - c + 15, 0, 30)
    NEG = np.float32(-1e30)
    out = np.full((DEPTH, 6, 128, 2, 15, 64), NEG, np.float32)
    for dr in range(15):
        Tb = np.where(colvalid[None, None], rpb[:, :, dr][:, :, dc], NEG)
        i0 = 14 - dr
        i1 = 15 - dr
        out[:, :, 0:64, 1, i0, :] = Tb
        if i1 <= 14:
            out[:, :, 64:128, 1, i1, :] = Tb
        if 3 <= dr <= 10:
            out[:, :, 0:64, 0, i0, :] = Tb
            out[:, :, 64:128, 0, i1, :] = Tb
    return out.reshape(DEPTH, 6, 128, 2 * 960)


def _bmask_tile(R, seq_rows, j, t):
    m = np.zeros((2, 64, 4, 64), np.float32)
    kt = 2 * j - 2 + t
    if 0 <= kt < R // 2:
        for qr in range(4):
            r = 4 * j + qr
            s = r // seq_rows
            rl = r - s * seq_rows
            r0 = min(max(rl - 4, 0), seq_rows - 8) + s * seq_rows
            for kr in range(2):
                key_r = 2 * kt + kr
                if r0 <= key_r < r0 + 8:
                    m[kr, :, qr, :] = 1.0
    return m.reshape(128, 256)


def _bmask_input(is_prompt_core):
    out = np.zeros((128, 36, 256), np.float32)
    idx = 0
    for j in B_SPECIAL[TP]:
        for t in range(6):
            out[:, idx] = _bmask_tile(64, 64 if is_prompt_core else 32, j, t)
            idx += 1
    for j in B_SPECIAL[TS]:
        for t in range(6):
            out[:, idx] = _bmask_tile(32, 32, j, t)
            idx += 1
    return out.reshape(128, 36 * 256).astype(ml_dtypes.bfloat16)


def _cmask_input():
    i = np.arange(128)[:, None]
    jq = np.arange(256)[None, :]
    out = np.zeros((128, 18, 2, 256), np.float32)
    for o in range(-8, 10):
        d = (jq - i) - 128 * o
        ad = np.abs(d)
        cnt = (ad <= 64).astype(np.float32) + ((d % 4 == 0) & (ad <= 256)) + ((d % 16 == 0) & (ad <= 1024))
        out[:, o + 8, 0] = cnt
        out[:, o + 8, 1] = cnt
    return out.reshape(128, 18 * 512).astype(ml_dtypes.bfloat16)


def _gains_input(norm_mix, norm_ffn, out_gain, q_gain, k_gain):
    g = np.zeros((128, 144), np.float32)
    for l in range(DEPTH):
        for k in range(8):
            g[:, l * 8 + k] = norm_mix[l, k * 128:(k + 1) * 128]
            g[:, 32 + l * 8 + k] = norm_ffn[l, k * 128:(k + 1) * 128]
            g[:, 64 + l * 8 + k] = out_gain[l, k * 128:(k + 1) * 128]
        for c in range(12):
            for hh in range(2):
                if c < 8:
                    head = 2 * c + hh
                    mix = 0 if head < 4 else (1 if head < 10 else 2)
                    g[hh * 64:(hh + 1) * 64, 96 + l * 12 + c] = q_gain[l, mix]
                else:
                    kv = 2 * (c - 8) + hh
                    mix = 0 if kv < 2 else (1 if kv < 5 else 2)
                    g[hh * 64:(hh + 1) * 64, 96 + l * 12 + c] = k_gain[l, mix]
    return g


_NC_CACHE = {}


def kernel(x_prompt, x_sample, norm_mix, w_in, q_gain, k_gain, rpb, out_gain, w_out, norm_ffn, w_gate_up, w_down):
    f = lambda a: np.ascontiguousarray(np.asarray(a, dtype=np.float32))
    x_prompt, x_sample = f(x_prompt), f(x_sample)
    w_in, w_out, w_gate_up, w_down = f(w_in), f(w_out), f(w_gate_up), f(w_down)
    rpb = f(rpb)
    w_in_t = np.ascontiguousarray(
        w_in[:, :, :1536].reshape(DEPTH, 8, 128, 12, 128).transpose(0, 3, 2, 1, 4)).reshape(DEPTH, 12, 128, 1024)
    w_v_t = np.ascontiguousarray(
        w_in[:, :, 1536:].reshape(DEPTH, 8, 128, 2, 256).transpose(0, 3, 2, 1, 4)).reshape(DEPTH, 2, 128, 2048)
    w_out_t = np.ascontiguousarray(
        w_out.reshape(DEPTH, 8, 128, 8, 128).transpose(0, 3, 2, 1, 4)).reshape(DEPTH, 8, 128, 1024)
    gu = w_gate_up.reshape(DEPTH, 8, 128, 2, NJ, 128)
    w_gu_t = np.ascontiguousarray(gu.transpose(0, 4, 2, 1, 3, 5)).reshape(DEPTH, NJ, 128, 2048)
    w_dn_t = np.ascontiguousarray(
        w_down.reshape(DEPTH, NJ, 128, 8, 128).transpose(0, 3, 2, 1, 4)).reshape(DEPTH, 8, 128, NJ * 128)
    gains = _gains_input(f(norm_mix), f(norm_ffn), f(out_gain), f(q_gain), f(k_gain))
    bbank = _bbank_input(rpb)
    rperm = _rperm_input()
    cmask = _cmask_input()
    shared = dict(w_in_t=w_in_t, w_v_t=w_v_t, w_out_t=w_out_t, w_gu_t=w_gu_t, w_dn_t=w_dn_t, gains=gains,
                  bbank=bbank, rperm=rperm, cmask=cmask)
    per_type = {}
    for ip in (True, False):
        per_type[ip] = dict(
            rope=_rope_input(ip), bmask=_bmask_input(ip),
            cross=np.full((128, 1), 0.0 if ip else -30000.0, np.float32))
    in_maps = []
    for c in range(8):
        if c < 4:
            toks = np.concatenate([x_prompt[c], x_sample[c]], axis=0)
        else:
            i = c - 4
            toks = np.concatenate([x_sample[8 + 2 * i], x_sample[9 + 2 * i], x_sample[4 + i]], axis=0)
        m = dict(shared)
        m.update(per_type[c < 4])
        m["xT"] = np.ascontiguousarray(toks.T)
        in_maps.append(m)
    if "nc" not in _NC_CACHE:
        _NC_CACHE["nc"] = build_nc()
    res = run_bass_kernel_spmd(_NC_CACHE["nc"], in_maps, core_ids=list(range(8)))
    y_prompt = np.zeros_like(x_prompt)
    y_sample = np.zeros_like(x_sample)
    for c in range(8):
        y = np.ascontiguousarray(res.results[c]["yT"].T)
        if c < 4:
            y_prompt[c] = y[:TP]
            y_sample[c] = y[TP:]
        else:
            i = c - 4
            y_sample[8 + 2 * i] = y[:TS]
            y_sample[9 + 2 * i] = y[TS:TP]
            y_sample[4 + i] = y[TP:]
    return (y_prompt, y_sample)
```

```python
import contextlib
import numpy as np
import ml_dtypes
import concourse.bass as bass
import concourse.mybir as mybir
from concourse.bass_utils import run_bass_kernel_spmd

F32 = mybir.dt.float32
BF = mybir.dt.bfloat16
AF = mybir.ActivationFunctionType
ALU = mybir.AluOpType

D = 1024
DEPTH = 4
NTOK = 6144
TP, TS = 4096, 2048
FFN = 2816
NJ = FFN // 128
EPS = 1e-6
SCALE = 0.125
SLOTS = ((0, TP), (TP, TS))
CHUNK_TYPE = {0: 0, 1: 0, 5: 1, 6: 1, 7: 1, 8: 0, 10: 2, 11: 1}
B_SPECIAL = {TP: (0, 7, 8, 15), TS: (0, 7)}
N_LAYERS_BUILD = DEPTH


class Prog:
    COMPUTE = ("pe", "act", "dve", "pool")
    EPOCH = 20000

    def __init__(self, nc):
        self.nc = nc
        self.ops = []
        self.last_w = {}
        self.readers = {}
        self.dma_count = {}

    def _add(self, eng, fn, reads, writes, dma_key=None):
        i = len(self.ops)
        deps = set()
        for b in reads:
            j = self.last_w.get(b)
            if j is not None:
                deps.add(j)
        for b in writes:
            j = self.last_w.get(b)
            if j is not None:
                deps.add(j)
            for j in self.readers.get(b, {}).values():
                deps.add(j)
        val = None
        if dma_key is not None:
            self.dma_count[dma_key] = self.dma_count.get(dma_key, 0) + 1
            val = 16 * self.dma_count[dma_key]
        self.ops.append([eng, fn, deps, dma_key, val, False])
        for b in writes:
            self.last_w[b] = i
            self.readers[b] = {}
        rk = eng if dma_key is None else ("dma", i)
        for b in reads:
            self.readers.setdefault(b, {})[rk] = i
        return i

    def op(self, eng, fn, reads=(), writes=()):
        return self._add(eng, fn, tuple(reads), tuple(writes))

    def dma(self, queue, fn, reads=(), writes=(), key=None):
        return self._add(queue, fn, tuple(reads), tuple(writes), dma_key=key)

    def emit(self, stack):
        nc = self.nc
        ops = self.ops
        for o in ops:
            for j in o[2]:
                pj = ops[j]
                if pj[3] is None:
                    if pj[0] == "pe" and o[0] == "pe" and o[3] is None:
                        continue
                    pj[5] = True
        sems = {}
        cnt = {e: 0 for e in self.COMPUTE}
        sig = {}
        for i, o in enumerate(ops):
            if o[3] is None:
                if o[5]:
                    cnt[o[0]] += 1
                    ep, v = divmod(cnt[o[0]] - 1, self.EPOCH)
                    sig[i] = ((o[0], ep), v + 1)
            else:
                sig[i] = (("dma", o[3]), o[4])
        for sk, _ in sig.values():
            if sk not in sems:
                sems[sk] = stack.enter_context(nc.semaphore("s%d" % len(sems)))
        per_eng = {}
        for i, o in enumerate(ops):
            per_eng.setdefault(o[0], []).append(i)
        engmap = {"pe": "tensor", "act": "scalar", "dve": "vector", "pool": "gpsimd", "sp": "sync"}
        final_waits = [(sems[("dma", k)], 16 * c) for k, c in self.dma_count.items()]
        block = stack.enter_context(nc.Block())

        def make(ename, idxs):
            def body(eng):
                waited = {}
                for i in idxs:
                    o = ops[i]
                    need = {}
                    for j in o[2]:
                        if j not in sig:
                            continue
                        sk, v = sig[j]
                        if v > need.get(sk, 0):
                            need[sk] = v
                    for sk, v in need.items():
                        if waited.get(sk, 0) >= v:
                            continue
                        eng.wait_ge(sems[sk], v)
                        waited[sk] = v
                    ins = o[1](eng)
                    if i in sig:
                        sk, v = sig[i]
                        ins.then_inc(sems[sk], 16 if o[3] is not None else 1)
                if ename == "sp":
                    for s, v in final_waits:
                        eng.wait_ge(s, v)
            return body

        if "sp" not in per_eng:
            per_eng["sp"] = []
        for ename, idxs in per_eng.items():
            getattr(block, engmap[ename])(make(ename, idxs))


class Rot:
    def __init__(self, items):
        self.items = list(items)
        self.i = -1

    def next(self):
        self.i = (self.i + 1) % len(self.items)
        return self.items[self.i]


def build_nc():
    nc = bass.Bass("TRN2", target_bir_lowering=False)

    def din(name, shape, dt=F32):
        return nc.dram_tensor(name, list(shape), dt, kind="ExternalInput").ap()

    def dscr(name, shape, dt):
        return nc.dram_tensor(name, list(shape), dt, kind="Internal").ap()

    xT = din("xT", [D, NTOK])
    yT = nc.dram_tensor("yT", [D, NTOK], F32, kind="ExternalOutput").ap()
    w_in_t = din("w_in_t", [DEPTH, 12, 128, 8 * 128])
    w_v_t = din("w_v_t", [DEPTH, 2, 128, 8 * 256])
    w_out_t = din("w_out_t", [DEPTH, 8, 128, 8 * 128])
    w_gu_t = din("w_gu_t", [DEPTH, NJ, 128, 8 * 256])
    w_dn_t = din("w_dn_t", [DEPTH, 8, 128, NJ * 128])
    gains_d = din("gains", [128, 3 * 32 + 48])
    rope_d = din("rope", [128, 6, NTOK])
    rperm_d = din("rperm", [128, 3 * 128], BF)
    bbank_d = din("bbank", [DEPTH, 6, 128, 2 * 960])
    bmask_d = din("bmask", [128, 36 * 256], BF)
    cmask_d = din("cmask", [128, 18 * 512], BF)
    cross_d = din("cross", [128, 1])

    wb_in = dscr("wb_in", [DEPTH, 12, 128, 8 * 128], BF)
    wb_v = dscr("wb_v", [DEPTH, 2, 128, 8 * 256], BF)
    wb_out = dscr("wb_out", [DEPTH, 8, 128, 8 * 128], BF)
    wb_gu = dscr("wb_gu", [DEPTH, NJ, 128, 8 * 256], BF)
    wb_dn = dscr("wb_dn", [DEPTH, 8, 128, NJ * 128], BF)
    xres = dscr("xres", [D, NTOK], F32)
    qkT = dscr("qkT", [12 * 128, NTOK], BF)
    Vd = dscr("Vd", [NTOK, 512], BF)
    oT = dscr("oT", [D, NTOK], F32)

    st = contextlib.ExitStack()
    with st:
        def sb(name, shape, dt):
            return st.enter_context(nc.sbuf_tensor("sb_" + name, list(shape), dt))

        def psum(name):
            return st.enter_context(nc.psum_tensor(name, [128, 512], F32))

        P = Prog(nc)
        uid = [0]

        def U(prefix):
            uid[0] += 1
            return (prefix, uid[0])

        gains = sb("gains", [128, 144], F32)
        rperm = sb("rperm", [128, 3, 128], BF)
        ones_bf = sb("ones_bf", [128, 128], BF)
        bones = sb("bones", [128, 128], BF)
        sel65 = sb("sel65", [65, 64], F32)
        cmask = sb("cmask", [128, 18, 512], BF)
        bmk = sb("bmk", [128, 6, 256], BF)

        P.dma("sp", lambda e: e.dma_start(out=gains[:], in_=gains_d), writes=["gains"], key="c_gains")
        P.dma("sp", lambda e: e.dma_start(out=rperm[:], in_=rperm_d.rearrange("p (a b) -> p a b", a=3)),
              writes=["rperm"], key="c_rperm")
        P.dma("sp", lambda e: e.dma_start(out=cmask[:], in_=cmask_d.rearrange("p (a b) -> p a b", a=18)),
              writes=["cmask"], key="c_cmask")
        P.op("pool", lambda e: e.memset(ones_bf[:], 1.0), writes=["ones_bf"])
        P.op("pool", lambda e: e.memset(bones[:], 0.0), writes=["bones"])
        P.op("pool", lambda e: e.memset(bones[0:64, 0:64], 1.0), reads=[], writes=["bones"])
        P.op("pool", lambda e: e.memset(bones[64:128, 64:128], 1.0), reads=[], writes=["bones"])
        P.op("pool", lambda e: e.memset(sel65[:], 0.0), writes=["sel65"])
        P.op("pool", lambda e: e.memset(sel65[64:65, :], 1.0), writes=["sel65"])

        def g_mix(l, k):
            return gains[:, l * 8 + k: l * 8 + k + 1]

        def g_ffn(l, k):
            return gains[:, 32 + l * 8 + k: 32 + l * 8 + k + 1]

        def g_out(l, k):
            return gains[:, 64 + l * 8 + k: 64 + l * 8 + k + 1]

        def g_qk(l, c):
            return gains[:, 96 + l * 12 + c: 96 + l * 12 + c + 1]

        def cast_w(src, dst, l, name):
            s2 = src[l]
            d2 = dst[l]
            nd = len(s2.shape)
            if nd == 3:
                s2 = s2.rearrange("a p n -> (a p) n")
                d2 = d2.rearrange("a p n -> (a p) n")
            rows = s2.shape[0]
            step = 512
            for r0 in range(0, rows, step):
                r1 = min(rows, r0 + step)
                P.dma("pool", lambda e, a=d2[r0:r1, :], b=s2[r0:r1, :]: e.dma_start(out=a, in_=b),
                      writes=[(name, l, r0 // 128 + i) for i in range((r1 - r0 + 127) // 128)],
                      key=("cast", name, l, r0))

        for l in range(N_LAYERS_BUILD):
            cast_w(w_in_t, wb_in, l, "wb_in")
            cast_w(w_v_t, wb_v, l, "wb_v")
            cast_w(w_out_t, wb_out, l, "wb_out")
            cast_w(w_gu_t, wb_gu, l, "wb_gu")
            cast_w(w_dn_t, wb_dn, l, "wb_dn")

        bank = [("ps%d" % i, psum("ps%d" % i)) for i in range(8)]
        psA = Rot(bank[0:3])
        psB = Rot(bank[3:5])
        psZ = bank[5]
        psN = bank[6]
        psH = bank[7]
        psS = Rot([bank[0], bank[1], bank[2], bank[6], bank[7]])

        xa = Rot([("xa%d" % i, sb("xa%d" % i, [128, 8, 512], F32)) for i in range(2)])
        hTr = Rot([("hT%d" % i, sb("hT%d" % i, [128, 8, 512], BF)) for i in range(2)])
        ropeR = Rot([("rope%d" % i, sb("rope%d" % i, [128, 6, 512], F32)) for i in range(2)])
        wsl = Rot([("w%d" % i, sb("w%d" % i, [128, 8 * 256], BF)) for i in range(3)])
        wdn = Rot([("wd%d" % i, sb("wd%d" % i, [128, NJ * 128], BF)) for i in range(2)])
        actT = sb("actT", [128, NJ, 512], BF)
        sqb = Rot([("sqb%d" % i, sb("sqb%d" % i, [128, 512], BF)) for i in range(2)])
        tf = Rot([("tf%d" % i, sb("tf%d" % i, [128, 512], F32)) for i in range(4)])
        tb = Rot([("tb%d" % i, sb("tb%d" % i, [128, 512], BF)) for i in range(5)])
        rstd3 = [("rs%d" % i, sb("rs%d" % i, [128, 512], F32)) for i in range(3)]
        sqc = Rot([("sqc%d" % i, sb("sqc%d" % i, [128, 512], BF)) for i in range(2)])
        stq = Rot([("stq%d" % i, sb("stq%d" % i, [128, 512], BF)) for i in range(2)])
        stv = Rot([("stv%d" % i, sb("stv%d" % i, [128, 4, 512], BF)) for i in range(1)])
        KT = Rot([("KT%d" % i, sb("KT%d" % i, [128, TP], BF)) for i in range(1)])
        QT = Rot([("QT%d" % i, sb("QT%d" % i, [128, 256], BF)) for i in range(3)])
        Vt = Rot([("Vt%d" % i, sb("Vt%d" % i, [128, 32, 128], BF)) for i in range(1)])
        ebR = Rot([("eb%d" % i, sb("eb%d" % i, [128, 2, 960], F32)) for i in range(2)])
        uf = Rot([("uf%d" % i, sb("uf%d" % i, [65, 512], F32)) for i in range(2)])
        ost = Rot([("ost%d" % i, sb("ost%d" % i, [64, 512], F32)) for i in range(1)])

        crossb = sb("crossb", [128, 1], F32)
        P.dma("sp", lambda e: e.dma_start(out=crossb[:], in_=cross_d), writes=["crossb"], key="c_crossb")
        for (vn, vt_) in Vt.items:
            P.op("pool", lambda e, t=vt_: e.memset(t[:, :, 64:128], 0.0), writes=[vn])
            P.op("pool", lambda e, t=vt_: e.memset(t[:, :, 64:65], 1.0), writes=[vn])

        class Pipe:
            def __init__(self):
                self.step = 0
                self.q = []
                self.seq = 0

            def defer(self, lag, fn):
                self.q.append((self.step + lag, self.seq, fn))
                self.seq += 1

            def tick(self):
                self.step += 1
                self.flush()

            def flush(self, everything=False):
                while True:
                    ready = [x for x in self.q if everything or x[0] <= self.step]
                    if not ready:
                        break
                    ready.sort()
                    x = ready[0]
                    self.q.remove(x)
                    x[2]()

        pipe = Pipe()

        def load_w(rot, src2d, deps, ncols):
            name, t = rot.next()
            P.dma("sp", lambda e, t=t, s=src2d, n=ncols: e.dma_start(out=t[:, 0:n], in_=s),
                  reads=deps, writes=[name], key="l_" + name)
            return name, t

        def recip(out_ap, in_ap, reads, writes, npart=128, n=512):
            P.op("dve", lambda e: e.reciprocal(out=out_ap, in_=in_ap), reads=list(reads), writes=list(writes))

        def rms_rstd(src_name, src_t, nk, groups, inv_dims):
            outs = []
            for gi, grp in enumerate(groups):
                pn, pt = psN
                for ii, k in enumerate(grp):
                    sn, s_ = sqb.next()
                    P.op("act", lambda e, s_=s_, k=k: e.activation(out=s_[:], in_=src_t[:, k, :], func=AF.Square),
                         reads=[(src_name, k)], writes=[sn])
                    P.op("pe", lambda e, s_=s_, a=(ii == 0), b=(ii == len(grp) - 1), pt=pt:
                         e.matmul(pt[:], ones_bf[:], s_[:], start=a, stop=b),
                         reads=[sn, "ones_bf"], writes=[pn])
                rn, rt = rstd3[gi]
                tn, tt_ = tf.next()
                P.op("act", lambda e, tt_=tt_, pt=pt, sc=inv_dims[gi]:
                     e.activation(out=tt_[:], in_=pt[:], func=AF.Ln, scale=sc, bias=EPS),
                     reads=[pn], writes=[tn])
                P.op("act", lambda e, tt_=tt_, rt=rt: e.activation(out=rt[:], in_=tt_[:], func=AF.Exp, scale=-0.5),
                     reads=[tn], writes=[rn])
                outs.append((rn, rt))
            return outs

        for l in range(N_LAYERS_BUILD):
            src = xT if l == 0 else xres
            dst = yT if l == N_LAYERS_BUILD - 1 else xres
            src_key = "xin" if l == 0 else "xres"
            dst_key = "yT" if l == N_LAYERS_BUILD - 1 else "xres"

            def prep(tc, l=l, src=src, src_key=src_key):
                t0 = tc * 512
                xn, xt = xa.next()
                P.dma("sp", lambda e, xt=xt, t0=t0, src=src: e.dma_start(
                    out=xt[:], in_=src[:, t0:t0 + 512].rearrange("(k p) t -> p k t", p=128)),
                    reads=[(src_key, tc)], writes=[(xn, k) for k in range(8)], key="l_" + xn)
                rpn, rpt = ropeR.next()
                P.dma("sp", lambda e, t0=t0, rpt=rpt: e.dma_start(out=rpt[:], in_=rope_d[:, :, t0:t0 + 512]),
                      writes=[rpn], key="l_" + rpn)
                (rn, rt), = rms_rstd(xn, xt, 8, [list(range(8))], [1.0 / D])
                hn, ht = hTr.next()
                for k in range(8):
                    P.op("dve", lambda e, k=k, xt=xt, rt=rt, l=l, ht=ht: e.scalar_tensor_tensor(
                        out=ht[:, k, :], in0=xt[:, k, :], scalar=g_mix(l, k), in1=rt[:],
                        op0=ALU.mult, op1=ALU.mult),
                        reads=[(xn, k), rn, "gains"], writes=[(hn, k)])
                return hn, ht, rpn, rpt

            def qk_chunk(tc, c, hn, ht, rpn, rpt, l=l):
                t0 = tc * 512
                wn, wt = load_w(wsl, wb_in[l, c], [("wb_in", l, c)], 1024)
                pn, pt = psA.next()
                for k in range(8):
                    P.op("pe", lambda e, k=k, wt=wt, pt=pt: e.matmul(
                        pt[:], wt[:, k * 128:(k + 1) * 128], ht[:, k, :], start=(k == 0), stop=(k == 7)),
                        reads=[wn, (hn, k)], writes=[pn])
                sn, s_ = sqc.next()
                P.op("act", lambda e, s_=s_, pt=pt: e.activation(out=s_[:], in_=pt[:], func=AF.Square),
                     reads=[pn], writes=[sn])
                ty = CHUNK_TYPE.get(c)
                state = {}

                def part2():
                    hhn, hht = psH
                    P.op("pe", lambda e, s_=s_, hht=hht: e.matmul(hht[:], bones[:], s_[:], start=True, stop=True),
                         reads=[sn, "bones"], writes=[hhn])
                    t1n, t1 = tf.next()
                    P.op("act", lambda e, t1=t1, hht=hht: e.activation(out=t1[:], in_=hht[:], func=AF.Ln,
                                                                       scale=1.0 / 64, bias=EPS),
                         reads=[hhn], writes=[t1n])
                    r2n, r2 = tf.next()
                    P.op("act", lambda e, t1=t1, r2=r2: e.activation(out=r2[:], in_=t1[:], func=AF.Exp, scale=-0.5),
                         reads=[t1n], writes=[r2n])
                    qsn, qs = stq.next()
                    state["qs"] = (qsn, qs)
                    if ty is None:
                        P.op("dve", lambda e, qs=qs, r2=r2: e.scalar_tensor_tensor(
                            out=qs[:], in0=pt[:], scalar=g_qk(l, c), in1=r2[:], op0=ALU.mult, op1=ALU.mult),
                            reads=[pn, r2n, "gains"], writes=[qsn])
                        store(qsn, qs)
                    else:
                        qfn, qf = tf.next()
                        P.op("dve", lambda e, qf=qf, r2=r2: e.scalar_tensor_tensor(
                            out=qf[:], in0=pt[:], scalar=g_qk(l, c), in1=r2[:], op0=ALU.mult, op1=ALU.mult),
                            reads=[pn, r2n, "gains"], writes=[qfn])
                        qbn, qb_ = tb.next()
                        P.op("act", lambda e, qb_=qb_, qf=qf: e.activation(out=qb_[:], in_=qf[:], func=AF.Copy),
                             reads=[qfn], writes=[qbn])
                        an, a_ = tf.next()
                        P.op("dve", lambda e, a_=a_, qf=qf: e.tensor_tensor(
                            out=a_[:], in0=qf[:], in1=rpt[:, 2 * ty, :], op=ALU.mult),
                            reads=[qfn, rpn], writes=[an])
                        state["rot"] = (qbn, qb_, an, a_)

                def store(qsn, qs):
                    P.dma("pool", lambda e, qs=qs: e.dma_start(
                        out=qkT[c * 128:(c + 1) * 128, t0:t0 + 512], in_=qs[:]),
                        reads=[qsn], writes=[("qkT", c, tc)], key="s_" + qsn)

                def part3():
                    qbn, qb_, an, a_ = state["rot"]
                    qsn, qs = state["qs"]
                    rbn, rb = psB.next()
                    P.op("pe", lambda e, rb=rb, qb_=qb_: e.matmul(rb[:], rperm[:, ty, :], qb_[:], start=True, stop=True),
                         reads=[qbn, "rperm"], writes=[rbn])
                    bn, b_ = tf.next()
                    P.op("dve", lambda e, b_=b_, rb=rb: e.tensor_tensor(
                        out=b_[:], in0=rb[:], in1=rpt[:, 2 * ty + 1, :], op=ALU.mult),
                        reads=[rbn, rpn], writes=[bn])
                    P.op("dve", lambda e, qs=qs, a_=a_, b_=b_: e.tensor_tensor(
                        out=qs[:], in0=a_[:], in1=b_[:], op=ALU.add),
                        reads=[an, bn], writes=[qsn])
                    store(qsn, qs)

                pipe.defer(2, part2)
                if ty is not None:
                    pipe.defer(3, part3)
                pipe.tick()

            def v_proj(tc, hn, ht, l=l):
                t0 = tc * 512
                svn, sv = stv.next()
                for half in range(2):
                    wn, wt = load_w(wsl, wb_v[l, half], [("wb_v", l, half)], 2048)
                    for tt in range(4):
                        pn, pt = psA.next()
                        for k in range(8):
                            P.op("pe", lambda e, k=k, tt=tt, pt=pt, wt=wt: e.matmul(
                                pt[:, 0:256], ht[:, k, tt * 128:(tt + 1) * 128], wt[:, k * 256:(k + 1) * 256],
                                start=(k == 0), stop=(k == 7)),
                                reads=[wn, (hn, k)], writes=[pn])
                        P.op("act", lambda e, sv=sv, tt=tt, pt=pt, half=half: e.activation(
                            out=sv[:, tt, half * 256:(half + 1) * 256], in_=pt[:, 0:256], func=AF.Copy),
                            reads=[pn], writes=[(svn, tt, half)])
                        pipe.tick()
                P.dma("pool", lambda e, sv=sv: e.dma_start(
                    out=Vd[t0:t0 + 512, :].rearrange("(tt p) n -> p tt n", p=128), in_=sv[:]),
                    reads=[(svn, tt, hf) for tt in range(4) for hf in range(2)], writes=[("Vd", tc)], key="s_" + svn)

            ntc = NTOK // 512
            cur = prep(0)
            for tc in range(ntc):
                hn, ht, rpn, rpt = cur
                for c in range(12):
                    qk_chunk(tc, c, hn, ht, rpn, rpt)
                    if c == 5 and tc + 1 < ntc:
                        cur = prep(tc + 1)
                v_proj(tc, hn, ht)
            pipe.flush(everything=True)

            def attend(base, T, g, heads, mixer, l=l):
                ntile = T // 128
                tcs = range(base // 512, (base + T) // 512)
                pipe.flush(everything=True)
                kn, kt_ = KT.next()
                for hh in range(2):
                    P.dma("sp", lambda e, kt_=kt_, hh=hh: e.dma_start(
                        out=kt_[hh * 64:(hh + 1) * 64, 0:T],
                        in_=qkT[1024 + g * 64:1024 + (g + 1) * 64, base:base + T]),
                        reads=[("qkT", 8 + g // 2, tc) for tc in tcs], writes=[kn], key="l_" + kn)
                vn, vt_ = Vt.next()
                P.dma("sp", lambda e, vt_=vt_: e.dma_start(
                    out=vt_[:, 0:ntile, 0:64],
                    in_=Vd[base:base + T, g * 64:(g + 1) * 64].rearrange("(n p) d -> p n d", p=128)),
                    reads=[("Vd", tc) for tc in tcs], writes=[vn], key="l_" + vn)
                ebs = []
                if mixer == 1:
                    for h in heads:
                        ebn, ebt = ebR.next()
                        P.dma("sp", lambda e, h=h, ebt=ebt: e.dma_start(
                            out=ebt[:], in_=bbank_d[l, h - 4].rearrange("p (a b) -> p a b", a=2)),
                            writes=[ebn], key="l_" + ebn)
                        P.op("act", lambda e, ebt=ebt: e.activation(out=ebt[:], in_=ebt[:], func=AF.Exp),
                             reads=[ebn], writes=[ebn])
                        ebs.append((ebn, ebt))
                N = 256
                for qb in range(T // N):
                    q0 = qb * N
                    qn, qt_ = QT.next()
                    P.dma("sp", lambda e, qt_=qt_, q0=q0: e.dma_start(
                        out=qt_[:, 0:N], in_=qkT[g * 128:(g + 1) * 128, base + q0:base + q0 + N]),
                        reads=[("qkT", g, (base + q0) // 512)], writes=[qn], key="l_" + qn)
                    tiles = []
                    if mixer == 0:
                        for kt in range(ntile):
                            cross = (T == TP) and ((q0 < TS) != (kt < 16))
                            tiles.append((kt, cross, None, None))
                    elif mixer == 2:
                        for kt in range(ntile):
                            o = kt - 2 * qb
                            if -8 <= o <= 9:
                                cross = (T == TP) and ((q0 < TS) != (kt < 16))
                                tiles.append((kt, cross, ("c", o + 8), None))
                    else:
                        spec = B_SPECIAL[T]
                        is_spec = qb in spec
                        if is_spec:
                            mi = (0 if T == TP else 4) + spec.index(qb)
                            P.dma("sp", lambda e, mi=mi: e.dma_start(
                                out=bmk[:], in_=bmask_d[:, mi * 6 * 256:(mi + 1) * 6 * 256].rearrange(
                                    "p (a b) -> p a b", a=6)),
                                writes=["bmk"], key="l_bmk")
                        for t in range(6):
                            kt = 2 * qb - 2 + t
                            if 0 <= kt < ntile:
                                tiles.append((kt, False, ("b", 1 if is_spec else 0, t), t if is_spec else None))
                    un, ut = psB.next()
                    for ti, (kt, cross, m1, m2) in enumerate(tiles):
                        sc = [psS.next(), psS.next()]
                        for hh in range(2):
                            P.op("pe", lambda e, s_=sc[hh][1], kt=kt, qt_=qt_, hh=hh: e.matmul(
                                s_[:, 0:256],
                                kt_[hh * 64:(hh + 1) * 64, kt * 128:(kt + 1) * 128],
                                qt_[hh * 64:(hh + 1) * 64, 0:256],
                                start=True, stop=True),
                                reads=[kn, qn], writes=[sc[hh][0]])
                        pbn, pb = tb.next()
                        if m1 is None:
                            dstn, dstt = pbn, pb
                        else:
                            dstn, dstt = tf.next()
                        for hh in range(2):
                            if cross:
                                P.op("act", lambda e, dstt=dstt, s_=sc[hh][1], hh=hh: e.activation(
                                    out=dstt[:, hh * 256:(hh + 1) * 256], in_=s_[:, 0:256], func=AF.Exp,
                                    scale=SCALE, bias=crossb[:, 0:1]),
                                    reads=[sc[hh][0], "crossb"], writes=[dstn])
                            else:
                                P.op("act", lambda e, dstt=dstt, s_=sc[hh][1], hh=hh: e.activation(
                                    out=dstt[:, hh * 256:(hh + 1) * 256], in_=s_[:, 0:256], func=AF.Exp,
                                    scale=SCALE),
                                    reads=[sc[hh][0]], writes=[dstn])
                        if m1 is not None:
                            ef = dstt
                            efn = dstn
                            if m1[0] == "c":
                                P.op("dve", lambda e, pb=pb, ef=ef, oi=m1[1]: e.tensor_tensor(
                                    out=pb[:], in0=ef[:], in1=cmask[:, oi, :], op=ALU.mult),
                                    reads=[efn, "cmask"], writes=[pbn])
                            else:
                                v, t = m1[1], m1[2]
                                i0 = (11 - 2 * t) * 64
                                if m2 is None:
                                    for hh in range(2):
                                        ebn, ebt = ebs[hh]
                                        P.op("dve", lambda e, pb=pb, ef=ef, v=v, i0=i0, hh=hh, ebt=ebt: e.tensor_tensor(
                                            out=pb[:, hh * 256:(hh + 1) * 256], in0=ef[:, hh * 256:(hh + 1) * 256],
                                            in1=ebt[:, v, i0:i0 + 256], op=ALU.mult),
                                            reads=[efn, ebn], writes=[(pbn, "h", hh)])
                                else:
                                    e2n, e2 = tf.next()
                                    for hh in range(2):
                                        ebn, ebt = ebs[hh]
                                        P.op("dve", lambda e, e2=e2, ef=ef, v=v, i0=i0, hh=hh, ebt=ebt: e.tensor_tensor(
                                            out=e2[:, hh * 256:(hh + 1) * 256], in0=ef[:, hh * 256:(hh + 1) * 256],
                                            in1=ebt[:, v, i0:i0 + 256], op=ALU.mult),
                                            reads=[efn, ebn], writes=[e2n])
                                    for hh in range(2):
                                        P.op("pool", lambda e, pb=pb, e2=e2, t=m2, hh=hh: e.tensor_tensor(
                                            out=pb[:, hh * 256:(hh + 1) * 256], in0=e2[:, hh * 256:(hh + 1) * 256],
                                            in1=bmk[:, t, :], op=ALU.mult),
                                            reads=[e2n, "bmk"], writes=[pbn])

                        def pv(ut=ut, un=un, kt=kt, pb=pb, pbn=pbn, a=(ti == 0), b=(ti == len(tiles) - 1)):
                            P.op("pe", lambda e: e.matmul(
                                ut[:, :], vt_[:, kt, :], pb[:], start=a, stop=b),
                                reads=[vn, pbn, (pbn, "h", 0), (pbn, "h", 1)], writes=[un])
                        pipe.defer(4, pv)
                        pipe.tick()

                    def fin1(ut=ut, un=un, q0=q0):
                        ufn, uft = uf.next()
                        P.op("act", lambda e: e.activation(out=uft[:], in_=ut[0:65, :], func=AF.Copy),
                             reads=[un], writes=[ufn])

                        def fin2():
                            zn, zt = psZ
                            P.op("pe", lambda e: e.matmul(zt[0:64, :], sel65[:], uft[:], start=True, stop=True),
                                 reads=[ufn, "sel65"], writes=[zn])
                            rzn, rz = tf.next()
                            if mixer == 0:
                                recip(rz[0:64, :], zt[0:64, :], [zn], [rzn], npart=64, n=512)
                            else:
                                P.op("act", lambda e: e.activation(out=rz[0:64, :], in_=zt[0:64, :], func=AF.Ln),
                                     reads=[zn], writes=[rzn])
                                P.op("act", lambda e: e.activation(out=rz[0:64, :], in_=rz[0:64, :], func=AF.Exp,
                                                                   scale=-1.0),
                                     reads=[rzn], writes=[rzn])
                            on, ot = ost.next()
                            P.op("pool", lambda e: e.tensor_tensor(
                                out=ot[:], in0=uft[0:64, :], in1=rz[0:64, :], op=ALU.mult),
                                reads=[ufn, rzn], writes=[on])
                            tcq = (base + q0) // 512
                            sub = (q0 // 256) % 2
                            P.dma("pool", lambda e: e.dma_start(
                                out=oT[2 * g * 64:(2 * g + 2) * 64, base + q0:base + q0 + 256].rearrange(
                                    "(h d) q -> d h q", h=2),
                                in_=ot[:].rearrange("d (h q) -> d h q", h=2)),
                                reads=[on], writes=[("oT", 2 * g, tcq, sub), ("oT", 2 * g + 1, tcq, sub)],
                                key="s_" + on)
                        pipe.defer(3, fin2)
                    pipe.defer(3, fin1)

            for (base, T) in SLOTS:
                for g in range(8):
                    mixer = 0 if g < 2 else (1 if g < 5 else 2)
                    attend(base, T, g, (2 * g, 2 * g + 1), mixer)
            pipe.flush(everything=True)

            for tc in range(NTOK // 512):
                t0 = tc * 512
                on_, ot_ = xa.next()
                oreads = []
                for h in range(16):
                    oreads += [("oT", h, tc, 0), ("oT", h, tc, 1)]
                P.dma("sp", lambda e, ot_=ot_, t0=t0: e.dma_start(
                    out=ot_[:], in_=oT[:, t0:t0 + 512].rearrange("(k p) t -> p k t", p=128)),
                    reads=oreads, writes=[(on_, k) for k in range(8)], key="l_" + on_)
                rs = rms_rstd(on_, ot_, 8, [[0, 1], [2, 3, 4], [5, 6, 7]], [1.0 / 256, 1.0 / 384, 1.0 / 384])
                hn, ht = hTr.next()
                grp_of = [0, 0, 1, 1, 1, 2, 2, 2]
                for k in range(8):
                    rn, rt = rs[grp_of[k]]
                    P.op("dve", lambda e, k=k, ot_=ot_, rt=rt, l=l, ht=ht: e.scalar_tensor_tensor(
                        out=ht[:, k, :], in0=ot_[:, k, :], scalar=g_out(l, k), in1=rt[:],
                        op0=ALU.mult, op1=ALU.mult),
                        reads=[(on_, k), rn, "gains"], writes=[(hn, k)])
                xn, xt = xa.next()
                P.dma("sp", lambda e, xt=xt, t0=t0, src=src: e.dma_start(
                    out=xt[:], in_=src[:, t0:t0 + 512].rearrange("(k p) t -> p k t", p=128)),
                    reads=[(src_key, tc)], writes=[(xn, k) for k in range(8)], key="l_" + xn)
                for j in range(8):
                    wn, wt = load_w(wsl, wb_out[l, j], [("wb_out", l, j)], 1024)
                    pn, pt = psA.next()
                    for k in range(8):
                        P.op("pe", lambda e, k=k, wt=wt, pt=pt, ht=ht: e.matmul(
                            pt[:], wt[:, k * 128:(k + 1) * 128], ht[:, k, :], start=(k == 0), stop=(k == 7)),
                            reads=[wn, (hn, k)], writes=[pn])
                    P.op("dve", lambda e, j=j, xt=xt, pt=pt: e.tensor_tensor(
                        out=xt[:, j, :], in0=xt[:, j, :], in1=pt[:], op=ALU.add),
                        reads=[(xn, j), pn], writes=[(xn, j)])
                (rn, rt), = rms_rstd(xn, xt, 8, [list(range(8))], [1.0 / D])
                hn, ht = hTr.next()
                for k in range(8):
                    P.op("dve", lambda e, k=k, xt=xt, rt=rt, l=l, ht=ht: e.scalar_tensor_tensor(
                        out=ht[:, k, :], in0=xt[:, k, :], scalar=g_ffn(l, k), in1=rt[:],
                        op0=ALU.mult, op1=ALU.mult),
                        reads=[(xn, k), rn, "gains"], writes=[(hn, k)])
                for j in range(NJ):
                    wn, wt = load_w(wsl, wb_gu[l, j], [("wb_gu", l, j)], 2048)
                    gn, gt = psA.next()
                    upn, upt = psB.next()
                    for k in range(8):
                        P.op("pe", lambda e, k=k, wt=wt, gt=gt, ht=ht: e.matmul(
                            gt[:], wt[:, k * 256:k * 256 + 128], ht[:, k, :], start=(k == 0), stop=(k == 7)),
                            reads=[wn, (hn, k)], writes=[gn])
                    for k in range(8):
                        P.op("pe", lambda e, k=k, wt=wt, upt=upt, ht=ht: e.matmul(
                            upt[:], wt[:, k * 256 + 128:k * 256 + 256], ht[:, k, :], start=(k == 0), stop=(k == 7)),
                            reads=[wn, (hn, k)], writes=[upn])
                    sgn, sg = tf.next()
                    P.op("act", lambda e, sg=sg, gt=gt: e.activation(out=sg[:], in_=gt[:], func=AF.Silu),
                         reads=[gn], writes=[sgn])
                    P.op("dve", lambda e, j=j, sg=sg, upt=upt: e.tensor_tensor(
                        out=actT[:, j, :], in0=sg[:], in1=upt[:], op=ALU.mult),
                        reads=[sgn, upn], writes=[("act", j)])
                for j in range(8):
                    wn, wt = load_w(wdn, wb_dn[l, j], [("wb_dn", l, j)], NJ * 128)
                    pn, pt = psA.next()
                    for k in range(NJ):
                        P.op("pe", lambda e, k=k, wt=wt, pt=pt: e.matmul(
                            pt[:], wt[:, k * 128:(k + 1) * 128], actT[:, k, :], start=(k == 0), stop=(k == NJ - 1)),
                            reads=[wn, ("act", k)], writes=[pn])
                    P.op("dve", lambda e, j=j, xt=xt, pt=pt: e.tensor_tensor(
                        out=xt[:, j, :], in0=xt[:, j, :], in1=pt[:], op=ALU.add),
                        reads=[(xn, j), pn], writes=[(xn, j)])
                P.dma("pool", lambda e, xt=xt, t0=t0, dst=dst: e.dma_start(
                    out=dst[:, t0:t0 + 512].rearrange("(k p) t -> p k t", p=128), in_=xt[:]),
                    reads=[(xn, k) for k in range(8)], writes=[(dst_key, tc)], key="s_" + xn)

        P.emit(st)
    return nc


def _rope_tab(pos, dim):
    inv = (np.float32(10000.0) ** (-(np.arange(0, dim, 2, dtype=np.float32)) / np.float32(dim))).astype(np.float32)
    ang = pos.astype(np.float32)[:, None] * inv[None, :]
    ang = np.concatenate([ang, ang], axis=-1)
    return np.cos(ang).astype(np.float32), np.sin(ang).astype(np.float32)


def _rope_input(is_prompt_core):
    posP = np.arange(TP) if is_prompt_core else (np.arange(TP) % TS)
    pos = np.concatenate([posP, np.arange(TS)]).astype(np.int64)
    cr, sr = _rope_tab(pos // 64, 32)
    cc, sc = _rope_tab(pos % 64, 32)
    c1, s1 = _rope_tab(pos, 64)
    sgn32 = np.where(np.arange(32) < 16, -1.0, 1.0).astype(np.float32)
    sgn64 = np.where(np.arange(64) < 32, -1.0, 1.0).astype(np.float32)
    cosA = np.concatenate([cr, cc], axis=1)
    sinA = np.concatenate([sr * sgn32, sc * sgn32], axis=1)
    cosC = c1
    sinC = s1 * sgn64
    out = np.zeros((128, 6, NTOK), np.float32)
    for hh in range(2):
        out[hh * 64:(hh + 1) * 64, 0] = cosA.T
        out[hh * 64:(hh + 1) * 64, 1] = sinA.T
        out[hh * 64:(hh + 1) * 64, 2] = cosC.T
        out[hh * 64:(hh + 1) * 64, 3] = sinC.T
    out[0:64, 4] = 1.0
    out[0:64, 5] = 0.0
    out[64:128, 4] = cosC.T
    out[64:128, 5] = sinC.T
    return out


def _rperm_input():
    R = np.zeros((128, 3, 128), np.float32)
    for dst in range(128):
        d = dst % 64
        base = dst - d
        hb = (d // 32) * 32
        i = d % 32
        ip = i + 16 if i < 16 else i - 16
        R[base + hb + ip, 0, dst] = 1.0
        dp = d + 32 if d < 32 else d - 32
        R[base + dp, 1, dst] = 1.0
        if dst >= 64:
            R[base + dp, 2, dst] = 1.0
    return R.reshape(128, 384).astype(ml_dtypes.bfloat16)


def _bbank_input(rpb):
    kc = np.arange(64)[:, None]
    c = np.arange(64)[None, :]
    c0 = np.clip(c - 8, 0, 48)
    colvalid = (kc >= c0) & (kc < c0 + 16)
    dc = np.clip(kc - c + 15, 0, 30)
    NEG = np.float32(-1e30)
    out = np.full((DEPTH, 6, 128, 2, 15, 64), NEG, np.float32)
    for dr in range(15):
        Tb = np.where(colvalid[None, None], rpb[:, :, dr][:, :, dc], NEG)
        i0 = 14 - dr
        i1 = 15 - dr
        out[:, :, 0:64, 1, i0, :] = Tb
        if i1 <= 14:
            out[:, :, 64:128, 1, i1, :] = Tb
        if 3 <= dr <= 10:
            out[:, :, 0:64, 0, i0, :] = Tb
            out[:, :, 64:128, 0, i1, :] = Tb
    return out.reshape(DEPTH, 6, 128, 2 * 960)


def _bmask_tile(R, seq_rows, j, t):
    m = np.zeros((2, 64, 4, 64), np.float32)
    kt = 2 * j - 2 + t
    if 0 <= kt < R // 2:
        for qr in range(4):
            r = 4 * j + qr
            s = r // seq_rows
            rl = r - s * seq_rows
            r0 = min(max(rl - 4, 0), seq_rows - 8) + s * seq_rows
            for kr in range(2):
                key_r = 2 * kt + kr
                if r0 <= key_r < r0 + 8:
                    m[kr, :, qr, :] = 1.0
    return m.reshape(128, 256)


def _bmask_input(is_prompt_core):
    out = np.zeros((128, 36, 256), np.float32)
    idx = 0
    for j in B_SPECIAL[TP]:
        for t in range(6):
            out[:, idx] = _bmask_tile(64, 64 if is_prompt_core else 32, j, t)
            idx += 1
    for j in B_SPECIAL[TS]:
        for t in range(6):
            out[:, idx] = _bmask_tile(32, 32, j, t)
            idx += 1
    return out.reshape(128, 36 * 256).astype(ml_dtypes.bfloat16)


def _cmask_input():
    i = np.arange(128)[:, None]
    jq = np.arange(256)[None, :]
    out = np.zeros((128, 18, 2, 256), np.float32)
    for o in range(-8, 10):
        d = (jq - i) - 128 * o
        ad = np.abs(d)
        cnt = (ad <= 64).astype(np.float32) + ((d % 4 == 0) & (ad <= 256)) + ((d % 16 == 0) & (ad <= 1024))
        out[:, o + 8, 0] = cnt
        out[:, o + 8, 1] = cnt
    return out.reshape(128, 18 * 512).astype(ml_dtypes.bfloat16)


def _gains_input(norm_mix, norm_ffn, out_gain, q_gain, k_gain):
    g = np.zeros((128, 144), np.float32)
    for l in range(DEPTH):
        for k in range(8):
            g[:, l * 8 + k] = norm_mix[l, k * 128:(k + 1) * 128]
            g[:, 32 + l * 8 + k] = norm_ffn[l, k * 128:(k + 1) * 128]
            g[:, 64 + l * 8 + k] = out_gain[l, k * 128:(k + 1) * 128]
        for c in range(12):
            for hh in range(2):
                if c < 8:
                    head = 2 * c + hh
                    mix = 0 if head < 4 else (1 if head < 10 else 2)
                    g[hh * 64:(hh + 1) * 64, 96 + l * 12 + c] = q_gain[l, mix]
                else:
                    kv = 2 * (c - 8) + hh
                    mix = 0 if kv < 2 else (1 if kv < 5 else 2)
                    g[hh * 64:(hh + 1) * 64, 96 + l * 12 + c] = k_gain[l, mix]
    return g


_NC_CACHE = {}


def kernel(x_prompt, x_sample, norm_mix, w_in, q_gain, k_gain, rpb, out_gain, w_out, norm_ffn, w_gate_up, w_down):
    f = lambda a: np.ascontiguousarray(np.asarray(a, dtype=np.float32))
    x_prompt, x_sample = f(x_prompt), f(x_sample)
    w_in, w_out, w_gate_up, w_down = f(w_in), f(w_out), f(w_gate_up), f(w_down)
    rpb = f(rpb)
    w_in_t = np.ascontiguousarray(
        w_in[:, :, :1536].reshape(DEPTH, 8, 128, 12, 128).transpose(0, 3, 2, 1, 4)).reshape(DEPTH, 12, 128, 1024)
    w_v_t = np.ascontiguousarray(
        w_in[:, :, 1536:].reshape(DEPTH, 8, 128, 2, 256).transpose(0, 3, 2, 1, 4)).reshape(DEPTH, 2, 128, 2048)
    w_out_t = np.ascontiguousarray(
        w_out.reshape(DEPTH, 8, 128, 8, 128).transpose(0, 3, 2, 1, 4)).reshape(DEPTH, 8, 128, 1024)
    gu = w_gate_up.reshape(DEPTH, 8, 128, 2, NJ, 128)
    w_gu_t = np.ascontiguousarray(gu.transpose(0, 4, 2, 1, 3, 5)).reshape(DEPTH, NJ, 128, 2048)
    w_dn_t = np.ascontiguousarray(
        w_down.reshape(DEPTH, NJ, 128, 8, 128).transpose(0, 3, 2, 1, 4)).reshape(DEPTH, 8, 128, NJ * 128)
    gains = _gains_input(f(norm_mix), f(norm_ffn), f(out_gain), f(q_gain), f(k_gain))
    bbank = _bbank_input(rpb)
    rperm = _rperm_input()
    cmask = _cmask_input()
    shared = dict(w_in_t=w_in_t, w_v_t=w_v_t, w_out_t=w_out_t, w_gu_t=w_gu_t, w_dn_t=w_dn_t, gains=gains,
                  bbank=bbank, rperm=rperm, cmask=cmask)
    per_type = {}
    for ip in (True, False):
        per_type[ip] = dict(
            rope=_rope_input(ip), bmask=_bmask_input(ip),
            cross=np.full((128, 1), 0.0 if ip else -30000.0, np.float32))
    in_maps = []
    for c in range(8):
        if c < 4:
            toks = np.concatenate([x_prompt[c], x_sample[c]], axis=0)
        else:
            i = c - 4
            toks = np.concatenate([x_sample[8 + 2 * i], x_sample[9 + 2 * i], x_sample[4 + i]], axis=0)
        m = dict(shared)
        m.update(per_type[c < 4])
        m["xT"] = np.ascontiguousarray(toks.T)
        in_maps.append(m)
    if "nc" not in _NC_CACHE:
        _NC_CACHE["nc"] = build_nc()
    res = run_bass_kernel_spmd(_NC_CACHE["nc"], in_maps, core_ids=list(range(8)))
    y_prompt = np.zeros_like(x_prompt)
    y_sample = np.zeros_like(x_sample)
    for c in range(8):
        y = np.ascontiguousarray(res.results[c]["yT"].T)
        if c < 4:
            y_prompt[c] = y[:TP]
            y_sample[c] = y[TP:]
        else:
            i = c - 4
            y_sample[8 + 2 * i] = y[:TS]
            y_sample[9 + 2 * i] = y[TS:TP]
            y_sample[4 + i] = y[TP:]
    return (y_prompt, y_sample)
```
